# Optimizing a Trainium2 kernel written in Bass

```python
import math
import jax, jax.numpy as jnp
from jax import lax
import numpy as np

D_MODEL = 2048
BATCH = 1
SEQ = 16384
DEPTH = 2

N_MEM = 256
RET_HEADS = 8
RET_DK = 128
RET_DV = 128
RET_CHUNK = 128
RET_W = RET_HEADS * RET_DV
DIFF_HEADS = 4
DIFF_DK = 128
DIFF_DV = 256
DIFF_QK_W = DIFF_HEADS * 2 * DIFF_DK
DIFF_W = DIFF_HEADS * DIFF_DV
Q_BLOCK = 128
MIX_WIDTH = RET_W + DIFF_W
IN_COLS = 4 * RET_W + 2 * DIFF_QK_W + DIFF_W
SPLIT_POINTS = (RET_W, 2 * RET_W, 3 * RET_W, 4 * RET_W,
                4 * RET_W + DIFF_QK_W, 4 * RET_W + 2 * DIFF_QK_W)
CONV_WIDTH = 31
XATTN_HEADS = 4
XATTN_DH = D_MODEL // XATTN_HEADS
D_FF = -(-8 * D_MODEL // (3 * 256)) * 256
N_EVEN = (DEPTH + 1) // 2
N_ODD = DEPTH // 2
DEEPNORM_ALPHA = (2.0 * DEPTH) ** 0.25
DEEPNORM_BETA = (8.0 * DEPTH) ** -0.25
LN_EPS = 1e-5

kernel_name = "hybrid_retention_diffattn_conformer_deepnorm"


def layer_norm(x, g, b):
    xf = x.astype(jnp.float32)
    mu = xf.mean(-1, keepdims=True)
    var = jnp.square(xf - mu).mean(-1, keepdims=True)
    return ((xf - mu) * lax.rsqrt(var + LN_EPS)).astype(x.dtype) * g + b


def head_layer_norm(x, g, b):
    B, S, H, d = x.shape
    xf = x.astype(jnp.float32)
    mu = xf.mean(-1, keepdims=True)
    var = jnp.square(xf - mu).mean(-1, keepdims=True)
    y = ((xf - mu) * lax.rsqrt(var + LN_EPS)).astype(x.dtype).reshape(B, S, H * d)
    return y * g + b


def head_rms_norm(x, g):
    B, S, H, d = x.shape
    xf = x.astype(jnp.float32)
    y = (xf * lax.rsqrt(jnp.mean(jnp.square(xf), -1, keepdims=True) + LN_EPS)).astype(x.dtype)
    return y.reshape(B, S, H * d) * g


def alibi_slopes(n_heads):
    return 2.0 ** (-8.0 * (jnp.arange(n_heads, dtype=jnp.float32) + 1.0) / n_heads)


def lambda_init_for(layer):
    return 0.8 - 0.6 * math.exp(-0.3 * layer)


def retention(q, k, v):
    B, S, H, dk = q.shape
    dv = v.shape[-1]
    C = RET_CHUNK
    nc = S // C
    dt = q.dtype
    log_gamma = jnp.log1p(-(2.0 ** (-5.0 - jnp.arange(H, dtype=jnp.float32))))
    k = k * (dk ** -0.5)
    q = q.reshape(B, nc, C, H, dk)
    k = k.reshape(B, nc, C, H, dk)
    v = v.reshape(B, nc, C, H, dv)
    pos = jnp.arange(C, dtype=jnp.float32)
    dist = pos[:, None] - pos[None, :]
    decay = jnp.where(dist >= 0,
                      jnp.exp(log_gamma[:, None, None] * jnp.maximum(dist, 0.0)),
                      0.0).astype(dt)
    scores = jnp.einsum('bnihd,bnjhd->bnhij', q, k) * decay
    y_intra = jnp.einsum('bnhij,bnjhe->bnihe', scores, v)
    zeta = jnp.exp(log_gamma[None, :] * (C - 1.0 - pos)[:, None]).astype(dt)
    kv = jnp.einsum('bnjhd,bnjhe->bnhde', k * zeta[:, :, None], v)
    chunk_decay = jnp.exp(log_gamma * C).astype(dt)[:, None, None]

    def step(state, kv_n):
        return state * chunk_decay + kv_n, state

    _, prev = lax.scan(step, jnp.zeros_like(kv[:, 0]), jnp.moveaxis(kv, 1, 0))
    prev = jnp.moveaxis(prev, 0, 1)
    xi = jnp.exp(log_gamma[None, :] * (pos + 1.0)[:, None]).astype(dt)
    y_cross = jnp.einsum('bnihd,bnhde->bnihe', q * xi[:, :, None], prev)
    return (y_intra + y_cross).reshape(B, S, H, dv)


def diff_attention(q, k, v, lam, slopes):
    B, S, H, _, dk = q.shape
    dv = v.shape[-1]
    nb = S // Q_BLOCK
    scale = dk ** -0.5
    qb = jnp.moveaxis(q.reshape(B, nb, Q_BLOCK, H, 2, dk), 1, 0)
    kpos = jnp.arange(S)

    def one_block(args):
        q_blk, blk = args
        qpos = blk * Q_BLOCK + jnp.arange(Q_BLOCK)
        s = jnp.einsum('bqhmd,bkhmd->bhmqk', q_blk, k).astype(jnp.float32) * scale
        dist = (qpos[:, None] - kpos[None, :]).astype(jnp.float32)
        s = s - slopes[None, :, None, None, None] * dist
        s = jnp.where(dist >= 0, s, -jnp.inf)
        p = jax.nn.softmax(s, axis=-1)
        a = p[:, :, 0] - lam * p[:, :, 1]
        return jnp.einsum('bhqk,bkhe->bqhe', a.astype(v.dtype), v)

    out = lax.map(one_block, (qb, jnp.arange(nb)))
    return jnp.moveaxis(out, 0, 1).reshape(B, S, H, dv)


def retention_diffattn_mixer(x, w_in, ret_gn_g, ret_gn_b, diff_lambda, diff_subln_g,
                             w_out, lambda_init):
    B, S, _ = x.shape
    proj = x @ w_in
    rq, rk, rv, rg, dq, dk_, dv_ = jnp.split(proj, SPLIT_POINTS, axis=-1)
    ret = retention(rq.reshape(B, S, RET_HEADS, RET_DK),
                    rk.reshape(B, S, RET_HEADS, RET_DK),
                    rv.reshape(B, S, RET_HEADS, RET_DV))
    ret = jax.nn.silu(rg) * head_layer_norm(ret, ret_gn_g, ret_gn_b)
    lf = diff_lambda.astype(jnp.float32)
    lam = jnp.exp(jnp.sum(lf[0] * lf[1])) - jnp.exp(jnp.sum(lf[2] * lf[3])) + lambda_init
    d = diff_attention(dq.reshape(B, S, DIFF_HEADS, 2, DIFF_DK),
                       dk_.reshape(B, S, DIFF_HEADS, 2, DIFF_DK),
                       dv_.reshape(B, S, DIFF_HEADS, DIFF_DV),
                       lam, alibi_slopes(DIFF_HEADS))
    d = head_rms_norm(d, diff_subln_g) * (1.0 - lambda_init)
    return jnp.concatenate([ret, d], axis=-1) @ w_out


def conformer_conv(x, w_pw1, b_pw1, w_dw, b_dw, ln_g, ln_b, w_pw2, b_pw2):
    D = x.shape[-1]
    h = x @ w_pw1 + b_pw1
    h = h[..., :D] * jax.nn.sigmoid(h[..., D:])
    h = lax.conv_general_dilated(
        h, w_dw[:, None, :].astype(h.dtype), window_strides=(1,),
        padding=[(CONV_WIDTH - 1, 0)],
        dimension_numbers=('NWC', 'WIO', 'NWC'),
        feature_group_count=D) + b_dw
    h = jax.nn.silu(layer_norm(h, ln_g, ln_b))
    return h @ w_pw2 + b_pw2


def memory_cross_attention(x, mem, wq, wk, wv, wo):
    B, S, D = x.shape
    M = mem.shape[1]
    q = (x @ wq).reshape(B, S, XATTN_HEADS, XATTN_DH)
    k = (mem @ wk).reshape(B, M, XATTN_HEADS, XATTN_DH)
    v = (mem @ wv).reshape(B, M, XATTN_HEADS, XATTN_DH)
    s = jnp.einsum('bshd,bmhd->bhsm', q, k).astype(jnp.float32) * (XATTN_DH ** -0.5)
    p = jax.nn.softmax(s, axis=-1).astype(v.dtype)
    o = jnp.einsum('bhsm,bmhd->bshd', p, v).reshape(B, S, D)
    return o @ wo


def swiglu_ffn(x, w_in, w_out):
    h = x @ w_in
    return (jax.nn.silu(h[..., :D_FF]) * h[..., D_FF:]) @ w_out


def setup_inputs(seed: int = 0) -> dict:
    key = jax.random.key(seed)
    ks = iter(jax.random.split(key, 32))

    def nrm(shape, scale):
        return jax.random.normal(next(ks), shape, jnp.float32) * scale

    def gain(shape):
        return 1.0 + nrm(shape, 0.02)

    D = D_MODEL
    return {
        "x": nrm((BATCH, SEQ, D), 1.0),
        "mem": nrm((BATCH, N_MEM, D), 1.0),
        "w_in": nrm((N_EVEN, D, IN_COLS), D ** -0.5),
        "ret_gn_g": gain((N_EVEN, RET_W)),
        "ret_gn_b": nrm((N_EVEN, RET_W), 0.02),
        "diff_lambda": nrm((N_EVEN, 4, DIFF_DK), 0.1),
        "diff_subln_g": gain((N_EVEN, DIFF_W)),
        "w_mix_out": nrm((N_EVEN, MIX_WIDTH, D), MIX_WIDTH ** -0.5 * DEEPNORM_BETA),
        "conv_w_pw1": nrm((N_ODD, D, 2 * D), D ** -0.5),
        "conv_b_pw1": nrm((N_ODD, 2 * D), 0.02),
        "conv_w_dw": nrm((N_ODD, CONV_WIDTH, D), CONV_WIDTH ** -0.5),
        "conv_b_dw": nrm((N_ODD, D), 0.02),
        "conv_ln_g": gain((N_ODD, D)),
        "conv_ln_b": nrm((N_ODD, D), 0.02),
        "conv_w_pw2": nrm((N_ODD, D, D), D ** -0.5 * DEEPNORM_BETA),
        "conv_b_pw2": nrm((N_ODD, D), 0.02),
        "xattn_wq": nrm((DEPTH, D, D), D ** -0.5),
        "xattn_wk": nrm((DEPTH, D, D), D ** -0.5),
        "xattn_wv": nrm((DEPTH, D, D), D ** -0.5),
        "xattn_wo": nrm((DEPTH, D, D), D ** -0.5 * DEEPNORM_BETA),
        "ffn_w_in": nrm((DEPTH, D, 2 * D_FF), D ** -0.5),
        "ffn_w_out": nrm((DEPTH, D_FF, D), D_FF ** -0.5 * DEEPNORM_BETA),
        "ln_g": gain((DEPTH, 3, D)),
        "ln_b": nrm((DEPTH, 3, D), 0.02),
    }


def reference(x, mem, w_in, ret_gn_g, ret_gn_b, diff_lambda, diff_subln_g, w_mix_out,
              conv_w_pw1, conv_b_pw1, conv_w_dw, conv_b_dw, conv_ln_g, conv_ln_b,
              conv_w_pw2, conv_b_pw2, xattn_wq, xattn_wk, xattn_wv, xattn_wo,
              ffn_w_in, ffn_w_out, ln_g, ln_b):
    h = x
    for layer in range(DEPTH):
        if layer % 2 == 0:
            e = layer // 2
            mix = retention_diffattn_mixer(h, w_in[e], ret_gn_g[e], ret_gn_b[e],
                                           diff_lambda[e], diff_subln_g[e], w_mix_out[e],
                                           lambda_init_for(layer + 1))
        else:
            o = layer // 2
            mix = conformer_conv(h, conv_w_pw1[o], conv_b_pw1[o], conv_w_dw[o], conv_b_dw[o],
                                 conv_ln_g[o], conv_ln_b[o], conv_w_pw2[o], conv_b_pw2[o])
        h = layer_norm(DEEPNORM_ALPHA * h + mix, ln_g[layer, 0], ln_b[layer, 0])
        xa = memory_cross_attention(h, mem, xattn_wq[layer], xattn_wk[layer],
                                    xattn_wv[layer], xattn_wo[layer])
        h = layer_norm(DEEPNORM_ALPHA * h + xa, ln_g[layer, 1], ln_b[layer, 1])
        ff = swiglu_ffn(h, ffn_w_in[layer], ffn_w_out[layer])
        h = layer_norm(DEEPNORM_ALPHA * h + ff, ln_g[layer, 2], ln_b[layer, 2])
    return h
```

```python
import math
from contextlib import ExitStack

import numpy as np
import concourse.bass as bass
import concourse.mybir as mybir
from concourse.bass_utils import run_bass_kernel_spmd

F32 = mybir.dt.float32
BF16 = mybir.dt.bfloat16
AF = mybir.ActivationFunctionType
ALU = mybir.AluOpType

D = 2048
KC = 16
SEQ = 16384
NCORES = 8
NMEM = 256
DFF = 5632
FC = DFF // 128
ALPHA = (2.0 * 2) ** 0.25
LN_EPS = 1e-5
LAMBDA_INIT = 0.8 - 0.6 * math.exp(-0.3 * 1)
XSCALE = 512 ** -0.5
HALO = 32


class Tk:
    __slots__ = ("w", "r", "name")

    def __init__(self, name=""):
        self.w = None
        self.r = []
        self.name = name


class V:
    __slots__ = ("ap", "tks")

    def __init__(self, ap, tks):
        self.ap = ap
        self.tks = tks


class Buf:
    def __init__(self, handle, nslots, name):
        self.h = handle
        self.n = nslots
        self.tks = [Tk(f"{name}[{i}]") for i in range(nslots)]

    def s(self, i, *idx):
        if idx:
            return V(self.h[(slice(None), i) + idx], [self.tks[i]])
        return V(self.h[:, i], [self.tks[i]])

    def rng(self, i0, i1, *idx):
        return V(self.h[(slice(None), slice(i0, i1)) + idx], self.tks[i0:i1])

    def all(self):
        return V(self.h[:], self.tks)


class Op:
    __slots__ = ("eng", "fn", "deps", "signal", "val", "is_dma", "sem", "inc")

    def __init__(self, eng, fn, is_dma):
        self.eng = eng
        self.fn = fn
        self.deps = []
        self.signal = False
        self.val = None
        self.is_dma = is_dma
        self.sem = None
        self.inc = 1


ENGS = ("pe", "act", "dve", "pool", "sp")
NDMA = {"sp": 12, "pool": 12, "act": 4}


class Sched:
    def __init__(self, nc, es):
        self.nc = nc
        self.ops = {e: [] for e in ENGS}
        self.esem = {e: es.enter_context(nc.semaphore(f"e_{e}")) for e in ENGS}
        self.dsem = {q: [es.enter_context(nc.semaphore(f"d_{q}{i}")) for i in range(n)]
                     for q, n in NDMA.items()}
        self.dcnt = {q: 0 for q in NDMA}
        self.dlast = {q: [None] * n for q, n in NDMA.items()}

    def _add(self, op, reads, writes):
        deps = {}

        def need(d, raw):
            if d is None or d is op:
                return
            if d.is_dma or op.is_dma:
                deps[id(d)] = d
            elif d.eng == op.eng:
                if raw and op.eng != "pe":
                    deps[id(d)] = d
            else:
                deps[id(d)] = d

        for v in reads:
            for tk in v.tks:
                need(tk.w, True)
        for v in writes:
            for tk in v.tks:
                need(tk.w, False)
                for r in tk.r:
                    need(r, False)
        for v in reads:
            for tk in v.tks:
                tk.r.append(op)
        for v in writes:
            for tk in v.tks:
                tk.w = op
                tk.r = []
        for d in deps.values():
            d.signal = True
            op.deps.append(d)
        self.ops[op.eng].append(op)
        return op

    def op(self, eng, fn, reads=(), writes=()):
        return self._add(Op(eng, fn, False), reads, writes)

    def dma(self, q, out_ap, in_ap, reads=(), writes=()):
        op = Op(q, lambda e: e.dma_start(out=out_ap, in_=in_ap), True)
        n = self.dcnt[q]
        self.dcnt[q] += 1
        k = n % NDMA[q]
        op.sem = self.dsem[q][k]
        op.val = 16 * (n // NDMA[q] + 1)
        op.inc = 16
        op.signal = True
        prev = self.dlast[q][k]
        self._add(op, reads, writes)
        if prev is not None:
            op.deps.append(prev)
        self.dlast[q][k] = op
        return op

    def mm(self, out, lhsT, rhs, start, stop):
        return self.op("pe", lambda e: e.matmul(out.ap, lhsT=lhsT.ap, rhs=rhs.ap, start=start, stop=stop),
                       reads=[lhsT, rhs], writes=[out])

    def act(self, out, in_, func, bias=None, scale=None, extra_reads=(), eng="act"):
        kw = {}
        if bias is not None:
            kw["bias"] = bias.ap if isinstance(bias, V) else bias
        if scale is not None:
            kw["scale"] = scale.ap if isinstance(scale, V) else scale
        rd = [in_] + [x for x in (bias, scale) if isinstance(x, V)] + list(extra_reads)
        return self.op(eng, lambda e: e.activation(out=out.ap, in_=in_.ap, func=func, **kw),
                       reads=rd, writes=[out])

    def tt(self, eng, out, in0, in1, op):
        return self.op(eng, lambda e: e.tensor_tensor(out=out.ap, in0=in0.ap, in1=in1.ap, op=op),
                       reads=[in0, in1], writes=[out])

    def ts(self, eng, out, in0, s1, s2, op0, op1=None):
        rd = [in0] + [x for x in (s1, s2) if isinstance(x, V)]
        a1 = s1.ap if isinstance(s1, V) else s1
        a2 = s2.ap if isinstance(s2, V) else s2
        if op1 is None:
            return self.op(eng, lambda e: e.tensor_scalar(out=out.ap, in0=in0.ap, scalar1=a1, scalar2=None, op0=op0),
                           reads=rd, writes=[out])
        return self.op(eng, lambda e: e.tensor_scalar(out=out.ap, in0=in0.ap, scalar1=a1, scalar2=a2, op0=op0, op1=op1),
                       reads=rd, writes=[out])

    def stt(self, eng, out, in0, scalar, in1, op0, op1):
        rd = [in0, in1] + ([scalar] if isinstance(scalar, V) else [])
        a = scalar.ap if isinstance(scalar, V) else scalar
        return self.op(eng, lambda e: e.scalar_tensor_tensor(out=out.ap, in0=in0.ap, scalar=a, in1=in1.ap, op0=op0, op1=op1),
                       reads=rd, writes=[out])

    def copy(self, eng, out, in_):
        return self.op(eng, lambda e: e.tensor_copy(out=out.ap, in_=in_.ap), reads=[in_], writes=[out])

    def memset(self, eng, out, val):
        return self.op(eng, lambda e: e.memset(out.ap, val), writes=[out])

    def emit(self, final_waits):
        nc = self.nc
        cnt = {e: 0 for e in ENGS}
        for e in ENGS:
            for op in self.ops[e]:
                if not op.is_dma:
                    op.sem = self.esem[e]
                    if op.signal:
                        cnt[e] += 1
                        op.val = cnt[e]
        ops = self.ops

        def run(eng_name, e):
            seen = {}
            for op in ops[eng_name]:
                for d in op.deps:
                    key = id(d.sem)
                    if seen.get(key, 0) < d.val:
                        e.wait_ge(d.sem, d.val)
                        seen[key] = d.val
                inst = op.fn(e)
                if op.signal:
                    inst.then_inc(op.sem, op.inc)
            if eng_name == "sp":
                for d in final_waits:
                    e.wait_ge(d.sem, d.val)

        with nc.Block() as block:
            @block.sync
            def _(e):
                run("sp", e)

            @block.tensor
            def _(e):
                run("pe", e)

            @block.scalar
            def _(e):
                run("act", e)

            @block.vector
            def _(e):
                run("dve", e)

            @block.gpsimd
            def _(e):
                run("pool", e)


class Arena:
    CELL = 256

    def __init__(self, handle, total):
        self.h = handle
        self.tks = [Tk(f"ar{i}") for i in range(total // self.CELL + 1)]

    def v(self, off, n, dt=BF16):
        nb = n * 2 if dt == F32 else n
        ap = self.h[:, off:off + nb]
        if dt == F32:
            ap = ap.bitcast(F32)
        return V(ap, self.tks[off // self.CELL:(off + nb - 1) // self.CELL + 1])


class Reg:
    def __init__(self, A, base, width, dt=BF16, stride=None):
        self.A, self.base, self.width, self.dt = A, base, width, dt
        self.es = 2 if dt == F32 else 1
        self.stride = (stride if stride is not None else width) * self.es

    def t(self, i, a=0, b=None):
        b = self.width if b is None else b
        return self.A.v(self.base + i * self.stride + a * self.es, b - a, self.dt)

    def multi(self, i0, n):
        v = self.A.v(self.base + i0 * self.stride, n * self.width, self.dt)
        return V(v.ap.rearrange("p (k c) -> p k c", c=self.width), v.tks)


class PsumPool:
    def __init__(self, handle):
        self.buf = Buf(handle, 8, "ps")
        self.i = 0

    nrot = 8

    def get(self, n=512):
        k = self.i % self.nrot
        self.i += 1
        return self.buf.s(k, slice(0, n))

    def v(self, k, a, b):
        return self.buf.s(k, slice(a, b))


class WStream:
    def __init__(self, S, A, base, nslots, slot_elems=8192):
        self.S, self.A, self.base, self.n, self.se = S, A, base, nslots, slot_elems
        self.i = 0

    def load(self, w2d, k0, nk, c0, ncols):
        assert nk * ncols <= self.se
        off = self.base + (self.i % self.n) * self.se
        self.i += 1
        slot = self.A.v(off, nk * ncols)
        out_ap = slot.ap.rearrange("p (k c) -> p k c", c=ncols)
        in_ap = w2d.rearrange("(k p) c -> p k c", p=128)[:, k0:k0 + nk, c0:c0 + ncols]
        self.S.dma("pool", out_ap, in_ap, writes=[slot])
        A = self.A

        def blk(k, j0, j1):
            return A.v(off + k * ncols + j0, j1 - j0)
        return blk


VO = {}
_c = 0
for _l in range(2):
    for _j in range(3):
        VO["lng", _l, _j] = _c; _c += 16
        VO["lnb", _l, _j] = _c; _c += 16
VO["b_pw1"] = _c; _c += 32
for _n in ("b_dw", "cln_g", "cln_b", "b_pw2"):
    VO[_n] = _c; _c += 16
VO["sub_g"] = _c; _c += 8
VO["w_dw"] = _c; _c += 31 * 16
VO["lam"] = _c; _c += 4
VO["hmask"] = _c; _c += 1
NV = _c


def fm(v):
    return np.ascontiguousarray(np.asarray(v, np.float32).reshape(-1, 128).T)


def pack_vecs(inp, hmask):
    out = np.zeros((128, NV), np.float32)
    for l in range(2):
        for j in range(3):
            out[:, VO["lng", l, j]:VO["lng", l, j] + 16] = fm(inp["ln_g"][l, j])
            out[:, VO["lnb", l, j]:VO["lnb", l, j] + 16] = fm(inp["ln_b"][l, j])
    out[:, VO["b_pw1"]:VO["b_pw1"] + 32] = fm(inp["conv_b_pw1"][0])
    out[:, VO["b_dw"]:VO["b_dw"] + 16] = fm(inp["conv_b_dw"][0])
    out[:, VO["cln_g"]:VO["cln_g"] + 16] = fm(inp["conv_ln_g"][0])
    out[:, VO["cln_b"]:VO["cln_b"] + 16] = fm(inp["conv_ln_b"][0])
    out[:, VO["b_pw2"]:VO["b_pw2"] + 16] = fm(inp["conv_b_pw2"][0])
    out[:, VO["sub_g"]:VO["sub_g"] + 8] = fm(inp["diff_subln_g"][0])
    for k in range(31):
        out[:, VO["w_dw"] + k * 16:VO["w_dw"] + (k + 1) * 16] = fm(inp["conv_w_dw"][0, k])
    out[:, VO["lam"]:VO["lam"] + 4] = np.asarray(inp["diff_lambda"][0], np.float32).T
    out[:, VO["hmask"]] = hmask
    return out


G = 512
T32 = 1024
OFF_H = 0
OFF_KV = OFF_H + 16 * T32
OFF_WB = OFF_KV + 4 * 4096
OFF_HB = OFF_WB + 3 * 8192
OFF_SQ = OFF_HB + 16 * G
OFF_ST = OFF_SQ + 4 * G
OFF_UT = OFF_ST + 4 * T32
OFF_BIG = OFF_UT + 1024
BIG_SZ = 50 * 512
AR_TOTAL = OFF_BIG + BIG_SZ


class P2:
    def __init__(self, S, A, PS, dram, vecs, consts, NG, do_l0, do_l1):
        self.S, self.A, self.PS, self.d, self.NG = S, A, PS, dram, NG
        self.do_l0, self.do_l1 = do_l0, do_l1
        self.VEC = vecs
        self.ONES, self.IDENT, self.ONESF, self.SM = consts
        self.H = Reg(A, OFF_H, G, F32)
        self.HB = Reg(A, OFF_HB, G)
        self.SQ = Reg(A, OFF_SQ, G)
        self.ST = Reg(A, OFF_ST, G, F32)
        self.BIG = Reg(A, OFF_BIG, G)
        self.KT = [Reg(A, OFF_KV + l * 8192, 256) for l in range(2)]
        self.VM = [Reg(A, OFF_KV + l * 8192 + 4096, 2048) for l in range(2)]
        self.WS = WStream(S, A, OFF_WB, 3)
        self.sqi = 0
        self.out_dmas = []

    def vcol(self, c, n=1):
        return V(self.VEC.h[:, c:c + n], self.VEC.tks)

    def layer_norm(self, gcol, bcol, src=None, dst32=None, dstb=None, silu=False):
        S, PS = self.S, self.PS
        src = src or self.H
        ps1, ps2 = PS.get(), PS.get()
        for c in range(KC):
            zb = self.SQ.t(self.sqi % 4); z2 = self.SQ.t((self.sqi + 1) % 4); self.sqi += 2
            S.act(zb, src.t(c), AF.Copy)
            S.act(z2, src.t(c), AF.Square)
            S.mm(ps1, self.ONES, zb, c == 0, c == KC - 1)
            S.mm(ps2, self.ONES, z2, c == 0, c == KC - 1)
        mean, msq, rstd = self.ST.t(0), self.ST.t(1), self.ST.t(2)
        S.ts("dve", mean, ps1, 1.0 / D, None, ALU.mult)
        S.tt("dve", msq, mean, mean, ALU.mult)
        S.stt("dve", rstd, ps2, 1.0 / D, msq, ALU.mult, ALU.subtract)
        S.ts("dve", rstd, rstd, 1.0, LN_EPS, ALU.mult, ALU.add)
        S.act(rstd, rstd, AF.Sqrt)
        S.op("dve", lambda e: e.reciprocal(out=rstd.ap, in_=rstd.ap), reads=[rstd], writes=[rstd])
        for c in range(KC):
            x = src.t(c)
            o32 = dst32.t(c) if dst32 is not None else x
            S.tt("dve", x, x, mean, ALU.subtract)
            S.tt("dve", x, x, rstd, ALU.mult)
            if silu:
                S.act(dstb.t(c), x, AF.Silu, scale=self.vcol(gcol + c), bias=self.vcol(bcol + c))
            else:
                S.act(o32, x, AF.Identity, scale=self.vcol(gcol + c), bias=self.vcol(bcol + c))
                S.act((dstb or self.HB).t(c), o32, AF.Copy)

    def linear(self, w2d, nk, col0, noc, rhs_fn, evac, n=G, cb=512):
        S, PS = self.S, self.PS
        per = cb // 128
        for b0 in range(0, noc, per):
            nb = min(per, noc - b0)
            blk = self.WS.load(w2d, 0, nk, col0 + b0 * 128, nb * 128)
            for j in range(nb):
                ps = PS.get(n)
                for k in range(nk):
                    S.mm(ps, blk(k, j * 128, (j + 1) * 128), rhs_fn(k), k == 0, k == nk - 1)
                evac(b0 + j, ps)

    def resid_evac(self, oc, ps):
        h = self.H.t(oc)
        self.S.stt("dve", h, h, ALPHA, ps, ALU.mult, ALU.add)

    def mem_kv(self, l, MB):
        S, PS = self.S, self.PS
        KT, VM = self.KT[l], self.VM[l]
        self.linear(self.d["xattn_wk"][l], KC, 0, KC, lambda k: MB.t(k),
                    lambda oc, ps: S.act(KT.t(oc), ps, AF.Copy), n=256)
        wv = self.d["xattn_wv"][l]
        for cbk in range(4):
            blk = self.WS.load(wv, 0, KC, cbk * 512, 512)
            for mt in range(2):
                ps = PS.get()
                for k in range(KC):
                    S.mm(ps, MB.t(k, mt * 128, (mt + 1) * 128), blk(k, 0, 512), k == 0, k == KC - 1)
                S.act(VM.t(mt, cbk * 512, (cbk + 1) * 512), ps, AF.Copy)

    def xattn(self, l):
        S, PS, BIG = self.S, self.PS, self.BIG
        KT, VM = self.KT[l], self.VM[l]
        self.linear(self.d["xattn_wq"][l], KC, 0, KC, lambda k: self.HB.t(k),
                    lambda oc, ps: S.act(BIG.t(oc), ps, AF.Copy))
        for hd in range(4):
            pts = []
            for mt in range(2):
                ps = PS.get()
                for c in range(4):
                    S.mm(ps, KT.t(hd * 4 + c, mt * 128, (mt + 1) * 128), BIG.t(hd * 4 + c), c == 0, c == 3)
                pt = BIG.t(32 + (hd % 2) * 2 + mt)
                S.act(pt, ps, AF.Exp, scale=XSCALE)
                pts.append(pt)
            psl = PS.get()
            for mt in range(2):
                S.mm(psl, self.ONES, pts[mt], mt == 0, mt == 1)
            rl = self.ST.t(3)
            S.op("dve", lambda e, rl=rl, psl=psl: e.reciprocal(out=rl.ap, in_=psl.ap), reads=[psl], writes=[rl])
            for c in range(4):
                ps = PS.get()
                for mt in range(2):
                    S.mm(ps, VM.t(mt, (hd * 4 + c) * 128, (hd * 4 + c + 1) * 128), pts[mt], mt == 0, mt == 1)
                S.tt("dve", BIG.t(16 + hd * 4 + c), ps, rl, ALU.mult)
        self.linear(self.d["xattn_wo"][l], KC, 0, KC, lambda k: BIG.t(16 + k), self.resid_evac)
        self.layer_norm(VO["lng", l, 1], VO["lnb", l, 1])

    def ffn(self, l):
        S, PS, BIG = self.S, self.PS, self.BIG
        w1 = self.d["ffn_w_in"][l]
        for b in range(FC // 4):
            blk1 = self.WS.load(w1, 0, KC, b * 512, 512)
            blk2 = self.WS.load(w1, 0, KC, DFF + b * 512, 512)
            for j in range(4):
                p1, p2 = PS.get(), PS.get()
                for k in range(KC):
                    S.mm(p1, blk1(k, j * 128, (j + 1) * 128), self.HB.t(k), k == 0, k == KC - 1)
                for k in range(KC):
                    S.mm(p2, blk2(k, j * 128, (j + 1) * 128), self.HB.t(k), k == 0, k == KC - 1)
                sl = self.ST.t(2 + (j % 2))
                S.act(sl, p1, AF.Silu)
                S.tt("dve", BIG.t(b * 4 + j), sl, p2, ALU.mult)
        self.linear(self.d["ffn_w_out"][l], FC, 0, KC, lambda k: BIG.t(k), self.resid_evac, cb=128)
        self.layer_norm(VO["lng", l, 2], VO["lnb", l, 2])

    def mix_tail(self, t0):
        S, PS, BIG, H = self.S, self.PS, self.BIG, self.H
        d = self.d
        fmv = lambda ap, n: ap.rearrange("(k p) t -> p k t", p=128)[:, 0:n, t0:t0 + G]
        S.dma("sp", H.multi(0, 8).ap, fmv(d["o0T"], 8), writes=[H.multi(0, 8)])
        S.dma("sp", H.multi(8, 8).ap, fmv(d["o1T"], 8), writes=[H.multi(8, 8)])
        S.dma("pool", BIG.multi(0, 8).ap, fmv(d["retT"], 8), writes=[BIG.multi(0, 8)])
        neglam = V(self.SM.h[:, 0:1], self.SM.tks)
        for hd in range(4):
            pss = PS.get()
            for j in range(2):
                c = hd * 2 + j
                S.stt("dve", H.t(c), H.t(8 + c), neglam, H.t(c), ALU.mult, ALU.add)
                z2 = self.SQ.t(self.sqi % 4); self.sqi += 1
                S.act(z2, H.t(c), AF.Square)
                S.mm(pss, self.ONES, z2, j == 0, j == 1)
            rstd = self.ST.t(hd % 2)
            S.ts("dve", rstd, pss, 1.0 / 256, LN_EPS, ALU.mult, ALU.add)
            S.act(rstd, rstd, AF.Sqrt)
            S.op("dve", lambda e, r=rstd: e.reciprocal(out=r.ap, in_=r.ap), reads=[rstd], writes=[rstd])
            for j in range(2):
                c = hd * 2 + j
                S.stt("dve", BIG.t(8 + c), H.t(c), self.vcol(VO["sub_g"] + c), rstd, ALU.mult, ALU.mult)
        S.dma("sp", H.multi(0, 16).ap, fmv(d["xT"], 16), writes=[H.multi(0, 16)])
        self.linear(d["w_mix_out"], KC, 0, KC, lambda k: BIG.t(k), self.resid_evac)
        self.layer_norm(VO["lng", 0, 0], VO["lnb", 0, 0])

    def conv_module(self, g, t0):
        S, PS, BIG, H, HB = self.S, self.PS, self.BIG, self.H, self.HB
        d = self.d
        UW = 32 + G
        UB = Reg(self.A, OFF_BIG, UW)
        DG = Reg(self.A, OFF_BIG + 16 * UW + 256, 128)
        Y2 = Reg(self.A, OFF_BIG + 16 * UW + 256 + 62 * 128 + 128, G, F32)
        UT = Reg(self.A, OFF_UT, 32)
        HH = Reg(self.A, OFF_UT + 512, 32)
        w1 = d["conv_w_pw1"]
        bcol = VO["b_pw1"]
        hmask = self.vcol(VO["hmask"])
        for b in range(4):
            blka = self.WS.load(w1, 0, KC, b * 512, 512)
            blkg = self.WS.load(w1, 0, KC, D + b * 512, 512)
            for j in range(4):
                oc = b * 4 + j
                pa, pg = PS.get(), PS.get()
                for k in range(KC):
                    S.mm(pa, blka(k, j * 128, (j + 1) * 128), HB.t(k), k == 0, k == KC - 1)
                for k in range(KC):
                    S.mm(pg, blkg(k, j * 128, (j + 1) * 128), HB.t(k), k == 0, k == KC - 1)
                sg = self.ST.t(2 + (j % 2))
                S.act(sg, pg, AF.Sigmoid, bias=self.vcol(bcol + 16 + oc))
                S.stt("dve", UB.t(oc, 32, UW), pa, self.vcol(bcol + oc), sg, ALU.add, ALU.mult)
                if g == 0:
                    pa2, pg2 = PS.get(32), PS.get(32)
                    for k in range(KC):
                        S.mm(pa2, blka(k, j * 128, (j + 1) * 128), HH.t(k), k == 0, k == KC - 1)
                    for k in range(KC):
                        S.mm(pg2, blkg(k, j * 128, (j + 1) * 128), HH.t(k), k == 0, k == KC - 1)
                    sg2 = self.ST.t(3, 0, 32)
                    S.act(sg2, pg2, AF.Sigmoid, bias=self.vcol(bcol + 16 + oc))
                    S.stt("dve", sg2, pa2, self.vcol(bcol + oc), sg2, ALU.add, ALU.mult)
                    S.ts("dve", UB.t(oc, 0, 32), sg2, hmask, None, ALU.mult)
                else:
                    S.copy("dve", UB.t(oc, 0, 32), UT.t(oc))
        Y1 = Reg(self.A, OFF_HB, G, F32)
        ytile = lambda c: Y1.t(c) if c < 8 else Y2.t(c - 8)
        for c in range(KC):
            ps = PS.get()
            for k in range(31):
                dg = DG.t((c % 2) * 31 + k)
                S.ts("dve", dg, self.IDENT, self.vcol(VO["w_dw"] + k * 16 + c), None, ALU.mult)
                S.mm(ps, dg, UB.t(c, 2 + k, 2 + k + G), k == 0, k == 30)
            S.act(ytile(c), ps, AF.Identity, bias=self.vcol(VO["b_dw"] + c))
            S.copy("dve", UT.t(c), UB.t(c, G, UW))

        class _Y:
            t = staticmethod(lambda c, *a: ytile(c))
        CN = Reg(self.A, OFF_BIG, G)
        self.layer_norm(VO["cln_g"], VO["cln_b"], src=_Y, dstb=CN, silu=True)
        def evac(oc, ps):
            h = H.t(oc)
            tmp = self.ST.t(2 + (oc % 2))
            S.act(tmp, ps, AF.Identity, bias=self.vcol(VO["b_pw2"] + oc))
            S.stt("dve", h, h, ALPHA, tmp, ALU.mult, ALU.add)
        self.linear(d["conv_w_pw2"], KC, 0, KC, lambda k: CN.t(k), evac)
        self.layer_norm(VO["lng", 1, 0], VO["lnb", 1, 0])

    def prologue(self):
        S, PS = self.S, self.PS
        S.memset("dve", self.ONES, 1.0)
        S.memset("dve", self.ONESF, 1.0)
        S.dma("pool", self.IDENT.ap, self.d["ident"], writes=[self.IDENT])
        S.dma("sp", self.VEC.h[:], self.d["vecs"], writes=[self.VEC.all()])
        if self.do_l0:
            pr = V(self.SM.h[:, 2:4], self.SM.tks)
            S.tt("dve", V(self.SM.h[:, 2:3], self.SM.tks), self.vcol(VO["lam"]), self.vcol(VO["lam"] + 1), ALU.mult)
            S.tt("dve", V(self.SM.h[:, 3:4], self.SM.tks), self.vcol(VO["lam"] + 2), self.vcol(VO["lam"] + 3), ALU.mult)
            ps = PS.get(2)
            S.mm(ps, self.ONESF, pr, True, True)
            ex = V(self.SM.h[:, 4:6], self.SM.tks)
            S.act(ex, ps, AF.Exp)
            nl = V(self.SM.h[:, 0:1], self.SM.tks)
            S.tt("dve", nl, V(self.SM.h[:, 5:6], self.SM.tks), V(self.SM.h[:, 4:5], self.SM.tks), ALU.subtract)
            S.ts("dve", nl, nl, -LAMBDA_INIT, None, ALU.add)
            sg = self.vcol(VO["sub_g"], 8)
            S.ts("dve", sg, sg, 1.0 - LAMBDA_INIT, None, ALU.mult)
        MB = Reg(self.A, OFF_BIG, 256)
        mbv = MB.multi(0, 16)
        S.dma("pool", mbv.ap, self.d["memT"].rearrange("(k p) m -> p k m", p=128), writes=[mbv])
        for l in ([0] if self.do_l0 else []) + ([1] if self.do_l1 else []):
            self.mem_kv(l, MB)

    def run(self):
        S, H, d = self.S, self.H, self.d
        self.prologue()
        for g in range(self.NG):
            t0 = g * G
            fmv = lambda ap: ap.rearrange("(k p) t -> p k t", p=128)[:, :, t0:t0 + G]
            if self.do_l0:
                self.mix_tail(t0)
                self.xattn(0)
                self.ffn(0)
            if self.do_l1:
                if not self.do_l0:
                    S.dma("sp", H.multi(0, 16).ap, fmv(d["h1T"]), writes=[H.multi(0, 16)])
                    for c in range(KC):
                        S.act(self.HB.t(c), H.t(c), AF.Copy)
                if g == 0:
                    HH = Reg(self.A, OFF_UT + 512, 32)
                    hv = HH.multi(0, 16)
                    S.dma("pool", hv.ap, d["haloT"].rearrange("(k p) t -> p k t", p=128), writes=[hv])
                self.conv_module(g, t0)
                self.xattn(1)
                self.ffn(1)
            outname = "outT" if self.do_l1 else "h1T_out"
            self.out_dmas.append(S.dma("sp", fmv(d[outname]), H.multi(0, 16).ap, reads=[H.multi(0, 16)]))


def build_p2(NG, do_l0, do_l1):
    nc = bass.Bass("TRN2", target_bir_lowering=False)
    NT = NG * G
    d = {}

    def inp(name, shape):
        d[name] = nc.dram_tensor(name, shape, F32, kind="ExternalInput").ap()

    inp("vecs", [128, NV]); inp("ident", [128, 128]); inp("memT", [D, NMEM])
    for n in ("xattn_wq", "xattn_wk", "xattn_wv", "xattn_wo"):
        inp(n, [2, D, D])
    inp("ffn_w_in", [2, D, 2 * DFF]); inp("ffn_w_out", [2, DFF, D])
    if do_l0:
        inp("xT", [D, NT]); inp("retT", [1024, NT]); inp("o0T", [1024, NT]); inp("o1T", [1024, NT])
        inp("w_mix_out", [D, D])
    if do_l1:
        inp("conv_w_pw1", [D, 2 * D]); inp("conv_w_pw2", [D, D]); inp("haloT", [D, 32])
        if not do_l0:
            inp("h1T", [D, NT])
    outname = "outT" if do_l1 else "h1T_out"
    d[outname] = nc.dram_tensor(outname, [D, NT], F32, kind="ExternalOutput").ap()
    with ExitStack() as es:
        ar = es.enter_context(nc.sbuf_tensor("ar", [128, AR_TOTAL], BF16))
        vec = es.enter_context(nc.sbuf_tensor("vec", [128, NV], F32))
        ones = es.enter_context(nc.sbuf_tensor("ones", [128, 128], BF16))
        ident = es.enter_context(nc.sbuf_tensor("identb", [128, 128], BF16))
        onesf = es.enter_context(nc.sbuf_tensor("onesf", [128, 128], F32))
        sm = es.enter_context(nc.sbuf_tensor("sm", [128, 8], F32))
        ps = es.enter_context(nc.psum_tensor("ps", [128, 8, 512], F32))
        S = Sched(nc, es)
        A = Arena(ar, AR_TOTAL)
        consts = (V(ones[:], [Tk("ones")]), V(ident[:], [Tk("ident")]), V(onesf[:], [Tk("onesf")]), Buf(sm, 1, "sm"))
        p2 = P2(S, A, PsumPool(ps), d, Buf(vec, 1, "vec"), consts, NG, do_l0, do_l1)
        p2.run()
        S.emit(p2.out_dmas)
    return nc


TB_OFF, TB_DIAG, TB_CT, TB_DM, TB_XI, TB_SC = 0, 512, 2560, 2688, 2816, 2944
NTAB = 2948


def p1_tables(core, inp):
    h = core
    gamma = 1.0 - 2.0 ** (-5.0 - h)
    lg = np.log1p(-(2.0 ** (-5.0 - h)))
    slope = 2.0 ** (-8.0 * (core // 2 + 1) / 4)
    tab = np.zeros((128, NTAB), np.float64)
    p = np.arange(128)[:, None]
    f = np.arange(512)[None, :]
    tab[:, TB_OFF:TB_OFF + 512] = -slope * (f - p)
    for r in range(4):
        dist = f - p - 128 * r
        tab[:, TB_DIAG + r * 512:TB_DIAG + (r + 1) * 512] = np.where(dist >= 0, -slope * dist, -30000.0)
    tab[:, TB_CT:TB_CT + 128] = -slope * 128.0 * np.arange(128)[None, :]
    i = np.arange(128)[None, :]
    tab[:, TB_DM:TB_DM + 128] = np.where(i - p >= 0, np.exp(lg * np.maximum(i - p, 0)), 0.0) * (128 ** -0.5)
    tab[:, TB_XI:TB_XI + 128] = np.exp(lg * (i + 1.0))
    tab[:, TB_SC + 0] = np.exp(lg * (127.0 - np.arange(128))) * (128 ** -0.5)
    tab[:, TB_SC + 1] = np.exp(lg * 128.0)
    tab[:, TB_SC + 2] = np.asarray(inp["ret_gn_g"][0, h * 128:(h + 1) * 128], np.float64)
    tab[:, TB_SC + 3] = np.asarray(inp["ret_gn_b"][0, h * 128:(h + 1) * 128], np.float64)
    return tab.astype(np.float32)


def p1_weights(core, w_in):
    h, hd, m = core, core // 2, core % 2
    sl = lambda a: w_in[:, a:a + 128]
    wf = np.concatenate([sl(h * 128), sl(1024 + h * 128), sl(3072 + h * 128),
                         sl(4096 + hd * 256 + m * 128), sl(5120 + hd * 256 + m * 128)], 1)
    wt = np.concatenate([sl(1024 + h * 128), sl(2048 + h * 128), w_in[:, 6144 + hd * 256:6144 + (hd + 1) * 256]], 1)
    return np.ascontiguousarray(wf), np.ascontiguousarray(wt)


class P1:
    parts = ('tm', 'ev1', 'ev2', 'ev3', 'ret', 'gn', 'gn2', 'att', 'att2')

    def __init__(self, S, A, PS, d, tab, ones, NTOK):
        self.S, self.A, self.PS, self.d, self.TAB, self.ONES, self.NTOK = S, A, PS, d, tab, ones, NTOK
        self.out_dmas = []

    def tb(self, c, n=1):
        return V(self.TAB.h[:, c:c + n], self.TAB.tks)

    def run(self):
        S, A, PS, d, NTOK = self.S, self.A, self.PS, self.d, self.NTOK
        o = 0
        def reg(n, width, dt=BF16):
            nonlocal o
            r = Reg(A, o, width, dt)
            o += n * width * (2 if dt == F32 else 1)
            return r
        DKT = reg(1, NTOK)
        DV = reg(NTOK // 128, 256)
        WF = reg(16, 640)
        WT = reg(16, 512)
        XB = reg(16, 512)
        RQ, RKF, DQ = reg(1, 512), reg(1, 512), reg(1, 512)
        RG, YR = reg(1, 512, F32), reg(1, 512, F32)
        RKZ, TM = reg(4, 128), reg(4, 512)
        SMT, QX = reg(2, 128), reg(2, 128)
        STATE = reg(1, 128, F32)
        STATEB = reg(1, 128)
        TT = reg(2, 512, F32)
        PP = reg(2, 512)
        YB = reg(2, 512)
        ST = reg(3, 512, F32)
        OO = reg(2, 512, F32)
        assert o <= 3 * NTOK + 43520, o
        PS.nrot = 5
        po0, po1, pl = PS.v(5, 0, 512), PS.v(6, 0, 512), PS.v(7, 0, 512)

        S.memset("dve", self.ONES, 1.0)
        S.dma("sp", self.TAB.h[:], d["tab"], writes=[self.TAB.all()])
        S.dma("pool", WF.multi(0, 16).ap, d["wf"].rearrange("(k p) c -> p k c", p=128), writes=[WF.multi(0, 16)])
        S.dma("pool", WT.multi(0, 16).ap, d["wt"].rearrange("(k p) c -> p k c", p=128), writes=[WT.multi(0, 16)])
        S.memset("dve", STATE.t(0), 0.0)
        S.memset("dve", STATEB.t(0), 0.0)
        QS = 128 ** -0.5
        for g in range(NTOK // 512):
            t0 = g * 512
            xv = XB.multi(0, 16)
            S.dma("pool", xv.ap, d["xT"].rearrange("(k p) t -> p k t", p=128)[:, :, t0:t0 + 512], writes=[xv])
            for j in range(5):
                ps = PS.get()
                for k in range(16):
                    S.mm(ps, WF.t(k, j * 128, (j + 1) * 128), XB.t(k), k == 0, k == 15)
                if j == 0:
                    S.act(RQ.t(0), ps, AF.Copy)
                elif j == 1:
                    S.act(RKF.t(0), ps, AF.Copy)
                elif j == 2:
                    S.act(RG.t(0), ps, AF.Silu)
                elif j == 3:
                    S.act(DQ.t(0), ps, AF.Copy, scale=QS)
                else:
                    S.act(DKT.t(0, t0, t0 + 512), ps, AF.Copy)
            for tt in (range(4) if 'tm' in self.parts else []):
                ps = PS.get()
                k_ = PS.i - 1
                for k in range(16):
                    S.mm(ps, XB.t(k, tt * 128, (tt + 1) * 128), WT.t(k), k == 0, k == 15)
                kk = k_ % PS.nrot
                S.act(TM.t(tt), ps, AF.Copy)
                S.ts("dve", RKZ.t(tt), TM.t(tt, 0, 128), self.tb(TB_SC + 0), None, ALU.mult)
                S.copy("dve", DV.t(4 * g + tt), TM.t(tt, 256, 512))
            for tt in (range(4) if 'ret' in self.parts else []):
                a, b = tt * 128, (tt + 1) * 128
                pss = PS.get(128)
                S.mm(pss, RKF.t(0, a, b), RQ.t(0, a, b), True, True)
                sm = SMT.t(tt % 2); qx = QX.t(tt % 2)
                S.tt("dve", sm, pss, self.tb(TB_DM, 128), ALU.mult)
                S.tt("dve", qx, RQ.t(0, a, b), self.tb(TB_XI, 128), ALU.mult)
                psy = PS.get(128)
                S.mm(psy, TM.t(tt, 128, 256), sm, True, False)
                S.mm(psy, STATEB.t(0), qx, False, True)
                S.act(YR.t(0, a, b), psy, AF.Copy)
                pkv = PS.get(128)
                S.mm(pkv, RKZ.t(tt), TM.t(tt, 128, 256), True, True)
                S.stt("dve", STATE.t(0), STATE.t(0), self.tb(TB_SC + 1), pkv, ALU.mult, ALU.add)
                S.act(STATEB.t(0), STATE.t(0), AF.Copy)
            y = YR.t(0)
            if 'ret' not in self.parts:
                S.memset('dve', y, 0.0)
            S.act(YB.t(0), y, AF.Copy)
            S.act(YB.t(1), y, AF.Square)
            ps1, ps2 = PS.get(), PS.get()
            S.mm(ps1, self.ONES, YB.t(0), True, True)
            S.mm(ps2, self.ONES, YB.t(1), True, True)
            mean, msq, rstd = ST.t(0), ST.t(1), ST.t(2)
            S.ts("dve", mean, ps1, 1.0 / 128, None, ALU.mult)
            if 'gn' not in self.parts:
                S.memset('dve', rstd, 1.0)
            S.tt("dve", msq, mean, mean, ALU.mult)
            S.stt("dve", rstd, ps2, 1.0 / 128, msq, ALU.mult, ALU.subtract)
            S.ts("dve", rstd, rstd, 1.0, LN_EPS, ALU.mult, ALU.add)
            S.act(rstd, rstd, AF.Sqrt)
            S.op("dve", lambda e, r=rstd: e.reciprocal(out=r.ap, in_=r.ap), reads=[rstd], writes=[rstd])
            S.tt("dve", y, y, mean, ALU.subtract)
            S.tt("dve", y, y, rstd, ALU.mult)
            S.act(y, y, AF.Identity, scale=self.tb(TB_SC + 2), bias=self.tb(TB_SC + 3))
            S.tt("dve", y, y, RG.t(0), ALU.mult)
            self.out_dmas.append(S.dma("sp", d["ret_out"][:, t0:t0 + 512], y.ap, reads=[y]))
            nj = 4 * g + 4
            if 'att2' not in self.parts:
                nj = 1
            if 'att' not in self.parts:
                nj = 0
            for j in range(nj):
                ps = PS.get()
                S.mm(ps, DKT.t(0, j * 128, (j + 1) * 128), DQ.t(0), True, True)
                r = j - 4 * g
                bt = self.tb(TB_OFF, 512) if r < 0 else self.tb(TB_DIAG + r * 512, 512)
                t = TT.t(j % 2); p = PP.t(j % 2)
                S.tt("dve", t, ps, bt, ALU.add)
                if r < 0:
                    S.act(p, t, AF.Exp, bias=self.tb(TB_CT + (4 * g - j)))
                else:
                    S.act(p, t, AF.Exp)
                S.mm(po0, DV.t(j, 0, 128), p, j == 0, j == nj - 1)
                S.mm(po1, DV.t(j, 128, 256), p, j == 0, j == nj - 1)
                S.mm(pl, self.ONES, p, j == 0, j == nj - 1)
            rl = ST.t(0)
            S.op("dve", lambda e, rl=rl: e.reciprocal(out=rl.ap, in_=pl.ap), reads=[pl], writes=[rl])
            S.tt("dve", OO.t(0), po0, rl, ALU.mult)
            S.tt("dve", OO.t(1), po1, rl, ALU.mult)
            ov = OO.multi(0, 2)
            self.out_dmas.append(S.dma("sp", d["o_out"].rearrange("(k p) t -> p k t", p=128)[:, :, t0:t0 + 512], ov.ap, reads=[ov]))


def build_p1(NTOK):
    nc = bass.Bass("TRN2", target_bir_lowering=False)
    d = {}
    for name, shape in (("xT", [D, NTOK]), ("wf", [D, 640]), ("wt", [D, 512]), ("tab", [128, NTAB])):
        d[name] = nc.dram_tensor(name, shape, F32, kind="ExternalInput").ap()
    p1o = nc.dram_tensor("p1_out", [384, NTOK], F32, kind="ExternalOutput").ap()
    d["ret_out"] = p1o[0:128, :]
    d["o_out"] = p1o[128:384, :]
    ar_total = 3 * NTOK + 43520
    with ExitStack() as es:
        ar = es.enter_context(nc.sbuf_tensor("ar", [128, ar_total], BF16))
        tab = es.enter_context(nc.sbuf_tensor("tabs", [128, NTAB], F32))
        ones = es.enter_context(nc.sbuf_tensor("ones", [128, 128], BF16))
        ps = es.enter_context(nc.psum_tensor("ps", [128, 8, 512], F32))
        S = Sched(nc, es)
        A = Arena(ar, ar_total)
        p1 = P1(S, A, PsumPool(ps), d, Buf(tab, 1, "tab"), V(ones[:], [Tk("ones")]), NTOK)
        p1.run()
        S.emit(p1.out_dmas)
    return nc


_NC_CACHE = {}


def _get(key, fn):
    if key not in _NC_CACHE:
        _NC_CACHE[key] = fn()
    return _NC_CACHE[key]


def kernel(**inp):
    inp = {k: np.asarray(v) for k, v in inp.items()}
    cores = list(range(NCORES))
    TPC = SEQ // NCORES
    NG = TPC // G
    xT = np.ascontiguousarray(inp["x"][0].T)
    memT = np.ascontiguousarray(inp["mem"][0].T)
    ident = np.eye(128, dtype=np.float32)

    nc1 = _get("p1", lambda: build_p1(SEQ))
    maps = []
    for c in cores:
        wf, wt = p1_weights(c, inp["w_in"][0])
        maps.append({"xT": xT, "wf": wf, "wt": wt, "tab": p1_tables(c, inp)})
    r1 = run_bass_kernel_spmd(nc1, maps, core_ids=cores).results
    retT = np.concatenate([r1[c]["p1_out"][0:128] for c in cores], 0)
    o0T = np.concatenate([r1[2 * h]["p1_out"][128:384] for h in range(4)], 0)
    o1T = np.concatenate([r1[2 * h + 1]["p1_out"][128:384] for h in range(4)], 0)
    del r1

    common = {"ident": ident, "memT": memT}
    for n in ("xattn_wq", "xattn_wk", "xattn_wv", "xattn_wo", "ffn_w_in", "ffn_w_out"):
        common[n] = inp[n]

    nc2 = _get("p2a", lambda: build_p2(NG, True, False))
    maps = []
    for c in cores:
        sl = slice(c * TPC, (c + 1) * TPC)
        m = dict(common)
        m.update(vecs=pack_vecs(inp, 1.0), xT=np.ascontiguousarray(xT[:, sl]), retT=np.ascontiguousarray(retT[:, sl]),
                 o0T=np.ascontiguousarray(o0T[:, sl]), o1T=np.ascontiguousarray(o1T[:, sl]), w_mix_out=inp["w_mix_out"][0])
        maps.append(m)
    r2 = run_bass_kernel_spmd(nc2, maps, core_ids=cores).results
    h1 = [r2[c]["h1T_out"] for c in cores]
    del r2

    nc3 = _get("p2b", lambda: build_p2(NG, False, True))
    maps = []
    for c in cores:
        m = dict(common)
        halo = h1[c - 1][:, TPC - HALO:] if c > 0 else np.zeros((D, HALO), np.float32)
        m.update(vecs=pack_vecs(inp, 1.0 if c > 0 else 0.0), h1T=h1[c], haloT=np.ascontiguousarray(halo),
                 conv_w_pw1=inp["conv_w_pw1"][0], conv_w_pw2=inp["conv_w_pw2"][0])
        maps.append(m)
    r3 = run_bass_kernel_spmd(nc3, maps, core_ids=cores).results
    outT = np.concatenate([r3[c]["outT"] for c in cores], 1)
    return np.ascontiguousarray(outT.T)[None].astype(np.float32)
```

```python
import math
from contextlib import ExitStack

import numpy as np
import concourse.bass as bass
import concourse.mybir as mybir
from concourse.bass_utils import run_bass_kernel_spmd

F32 = mybir.dt.float32
BF16 = mybir.dt.bfloat16
AF = mybir.ActivationFunctionType
ALU = mybir.AluOpType

D = 2048
KC = 16
SEQ = 16384
NCORES = 8
NMEM = 256
DFF = 5632
FC = DFF // 128
ALPHA = (2.0 * 2) ** 0.25
LN_EPS = 1e-5
LAMBDA_INIT = 0.8 - 0.6 * math.exp(-0.3 * 1)
XSCALE = 512 ** -0.5
HALO = 32


class Tk:
    __slots__ = ("w", "r", "name")

    def __init__(self, name=""):
        self.w = None
        self.r = []
        self.name = name


class V:
    __slots__ = ("ap", "tks")

    def __init__(self, ap, tks):
        self.ap = ap
        self.tks = tks


class Buf:
    def __init__(self, handle, nslots, name):
        self.h = handle
        self.n = nslots
        self.tks = [Tk(f"{name}[{i}]") for i in range(nslots)]

    def s(self, i, *idx):
        if idx:
            return V(self.h[(slice(None), i) + idx], [self.tks[i]])
        return V(self.h[:, i], [self.tks[i]])

    def rng(self, i0, i1, *idx):
        return V(self.h[(slice(None), slice(i0, i1)) + idx], self.tks[i0:i1])

    def all(self):
        return V(self.h[:], self.tks)


class Op:
    __slots__ = ("eng", "fn", "deps", "signal", "val", "is_dma", "sem", "inc")

    def __init__(self, eng, fn, is_dma):
        self.eng = eng
        self.fn = fn
        self.deps = []
        self.signal = False
        self.val = None
        self.is_dma = is_dma
        self.sem = None
        self.inc = 1


ENGS = ("pe", "act", "dve", "pool", "sp")
NDMA = {"sp": 12, "pool": 12, "act": 4}


class Sched:
    def __init__(self, nc, es):
        self.nc = nc
        self.ops = {e: [] for e in ENGS}
        self.esem = {e: es.enter_context(nc.semaphore(f"e_{e}")) for e in ENGS}
        self.dsem = {q: [es.enter_context(nc.semaphore(f"d_{q}{i}")) for i in range(n)]
                     for q, n in NDMA.items()}
        self.dcnt = {q: 0 for q in NDMA}
        self.dlast = {q: [None] * n for q, n in NDMA.items()}

    def _add(self, op, reads, writes):
        deps = {}

        def need(d, raw):
            if d is None or d is op:
                return
            if d.is_dma or op.is_dma:
                deps[id(d)] = d
            elif d.eng == op.eng:
                if raw and op.eng != "pe":
                    deps[id(d)] = d
            else:
                deps[id(d)] = d

        for v in reads:
            for tk in v.tks:
                need(tk.w, True)
        for v in writes:
            for tk in v.tks:
                need(tk.w, False)
                for r in tk.r:
                    need(r, False)
        for v in reads:
            for tk in v.tks:
                tk.r.append(op)
        for v in writes:
            for tk in v.tks:
                tk.w = op
                tk.r = []
        for d in deps.values():
            d.signal = True
            op.deps.append(d)
        self.ops[op.eng].append(op)
        return op

    def op(self, eng, fn, reads=(), writes=()):
        return self._add(Op(eng, fn, False), reads, writes)

    def dma(self, q, out_ap, in_ap, reads=(), writes=()):
        op = Op(q, lambda e: e.dma_start(out=out_ap, in_=in_ap), True)
        n = self.dcnt[q]
        self.dcnt[q] += 1
        k = n % NDMA[q]
        op.sem = self.dsem[q][k]
        op.val = 16 * (n // NDMA[q] + 1)
        op.inc = 16
        op.signal = True
        prev = self.dlast[q][k]
        self._add(op, reads, writes)
        if prev is not None:
            op.deps.append(prev)
        self.dlast[q][k] = op
        return op

    def mm(self, out, lhsT, rhs, start, stop):
        return self.op("pe", lambda e: e.matmul(out.ap, lhsT=lhsT.ap, rhs=rhs.ap, start=start, stop=stop),
                       reads=[lhsT, rhs], writes=[out])

    def act(self, out, in_, func, bias=None, scale=None, extra_reads=(), eng="act"):
        kw = {}
        if bias is not None:
            kw["bias"] = bias.ap if isinstance(bias, V) else bias
        if scale is not None:
            kw["scale"] = scale.ap if isinstance(scale, V) else scale
        rd = [in_] + [x for x in (bias, scale) if isinstance(x, V)] + list(extra_reads)
        return self.op(eng, lambda e: e.activation(out=out.ap, in_=in_.ap, func=func, **kw),
                       reads=rd, writes=[out])

    def tt(self, eng, out, in0, in1, op):
        return self.op(eng, lambda e: e.tensor_tensor(out=out.ap, in0=in0.ap, in1=in1.ap, op=op),
                       reads=[in0, in1], writes=[out])

    def ts(self, eng, out, in0, s1, s2, op0, op1=None):
        rd = [in0] + [x for x in (s1, s2) if isinstance(x, V)]
        a1 = s1.ap if isinstance(s1, V) else s1
        a2 = s2.ap if isinstance(s2, V) else s2
        if op1 is None:
            return self.op(eng, lambda e: e.tensor_scalar(out=out.ap, in0=in0.ap, scalar1=a1, scalar2=None, op0=op0),
                           reads=rd, writes=[out])
        return self.op(eng, lambda e: e.tensor_scalar(out=out.ap, in0=in0.ap, scalar1=a1, scalar2=a2, op0=op0, op1=op1),
                       reads=rd, writes=[out])

    def stt(self, eng, out, in0, scalar, in1, op0, op1):
        rd = [in0, in1] + ([scalar] if isinstance(scalar, V) else [])
        a = scalar.ap if isinstance(scalar, V) else scalar
        return self.op(eng, lambda e: e.scalar_tensor_tensor(out=out.ap, in0=in0.ap, scalar=a, in1=in1.ap, op0=op0, op1=op1),
                       reads=rd, writes=[out])

    def copy(self, eng, out, in_):
        return self.op(eng, lambda e: e.tensor_copy(out=out.ap, in_=in_.ap), reads=[in_], writes=[out])

    def memset(self, eng, out, val):
        return self.op(eng, lambda e: e.memset(out.ap, val), writes=[out])

    def emit(self, final_waits):
        nc = self.nc
        cnt = {e: 0 for e in ENGS}
        for e in ENGS:
            for op in self.ops[e]:
                if not op.is_dma:
                    op.sem = self.esem[e]
                    if op.signal:
                        cnt[e] += 1
                        op.val = cnt[e]
        ops = self.ops

        def run(eng_name, e):
            seen = {}
            for op in ops[eng_name]:
                for d in op.deps:
                    key = id(d.sem)
                    if seen.get(key, 0) < d.val:
                        e.wait_ge(d.sem, d.val)
                        seen[key] = d.val
                inst = op.fn(e)
                if op.signal:
                    inst.then_inc(op.sem, op.inc)
            if eng_name == "sp":
                for d in final_waits:
                    e.wait_ge(d.sem, d.val)

        with nc.Block() as block:
            @block.sync
            def _(e):
                run("sp", e)

            @block.tensor
            def _(e):
                run("pe", e)

            @block.scalar
            def _(e):
                run("act", e)

            @block.vector
            def _(e):
                run("dve", e)

            @block.gpsimd
            def _(e):
                run("pool", e)


class Arena:
    CELL = 256

    def __init__(self, handle, total):
        self.h = handle
        self.tks = [Tk(f"ar{i}") for i in range(total // self.CELL + 1)]

    def v(self, off, n, dt=BF16):
        nb = n * 2 if dt == F32 else n
        ap = self.h[:, off:off + nb]
        if dt == F32:
            ap = ap.bitcast(F32)
        return V(ap, self.tks[off // self.CELL:(off + nb - 1) // self.CELL + 1])


class Reg:
    def __init__(self, A, base, width, dt=BF16, stride=None):
        self.A, self.base, self.width, self.dt = A, base, width, dt
        self.es = 2 if dt == F32 else 1
        self.stride = (stride if stride is not None else width) * self.es

    def t(self, i, a=0, b=None):
        b = self.width if b is None else b
        return self.A.v(self.base + i * self.stride + a * self.es, b - a, self.dt)

    def multi(self, i0, n):
        v = self.A.v(self.base + i0 * self.stride, n * self.width, self.dt)
        return V(v.ap.rearrange("p (k c) -> p k c", c=self.width), v.tks)


class PsumPool:
    def __init__(self, handle):
        self.buf = Buf(handle, 8, "ps")
        self.i = 0

    nrot = 8

    def get(self, n=512):
        k = self.i % self.nrot
        self.i += 1
        return self.buf.s(k, slice(0, n))

    def v(self, k, a, b):
        return self.buf.s(k, slice(a, b))


class WStream:
    def __init__(self, S, A, base, nslots, slot_elems=8192):
        self.S, self.A, self.base, self.n, self.se = S, A, base, nslots, slot_elems
        self.i = 0

    def load(self, w2d, k0, nk, c0, ncols):
        assert nk * ncols <= self.se
        off = self.base + (self.i % self.n) * self.se
        self.i += 1
        slot = self.A.v(off, nk * ncols)
        out_ap = slot.ap.rearrange("p (k c) -> p k c", c=ncols)
        in_ap = w2d.rearrange("(k p) c -> p k c", p=128)[:, k0:k0 + nk, c0:c0 + ncols]
        self.S.dma("pool", out_ap, in_ap, writes=[slot])
        A = self.A

        def blk(k, j0, j1):
            return A.v(off + k * ncols + j0, j1 - j0)
        return blk


VO = {}
_c = 0
for _l in range(2):
    for _j in range(3):
        VO["lng", _l, _j] = _c; _c += 16
        VO["lnb", _l, _j] = _c; _c += 16
VO["b_pw1"] = _c; _c += 32
for _n in ("b_dw", "cln_g", "cln_b", "b_pw2"):
    VO[_n] = _c; _c += 16
VO["sub_g"] = _c; _c += 8
VO["w_dw"] = _c; _c += 31 * 16
VO["lam"] = _c; _c += 4
VO["hmask"] = _c; _c += 1
NV = _c


def fm(v):
    return np.ascontiguousarray(np.asarray(v, np.float32).reshape(-1, 128).T)


def pack_vecs(inp, hmask):
    out = np.zeros((128, NV), np.float32)
    for l in range(2):
        for j in range(3):
            out[:, VO["lng", l, j]:VO["lng", l, j] + 16] = fm(inp["ln_g"][l, j])
            out[:, VO["lnb", l, j]:VO["lnb", l, j] + 16] = fm(inp["ln_b"][l, j])
    out[:, VO["b_pw1"]:VO["b_pw1"] + 32] = fm(inp["conv_b_pw1"][0])
    out[:, VO["b_dw"]:VO["b_dw"] + 16] = fm(inp["conv_b_dw"][0])
    out[:, VO["cln_g"]:VO["cln_g"] + 16] = fm(inp["conv_ln_g"][0])
    out[:, VO["cln_b"]:VO["cln_b"] + 16] = fm(inp["conv_ln_b"][0])
    out[:, VO["b_pw2"]:VO["b_pw2"] + 16] = fm(inp["conv_b_pw2"][0])
    out[:, VO["sub_g"]:VO["sub_g"] + 8] = fm(inp["diff_subln_g"][0])
    for k in range(31):
        out[:, VO["w_dw"] + k * 16:VO["w_dw"] + (k + 1) * 16] = fm(inp["conv_w_dw"][0, k])
    out[:, VO["lam"]:VO["lam"] + 4] = np.asarray(inp["diff_lambda"][0], np.float32).T
    out[:, VO["hmask"]] = hmask
    return out


G = 512
T32 = 1024
OFF_H = 0
OFF_KV = OFF_H + 16 * T32
OFF_WB = OFF_KV + 4 * 4096
OFF_HB = OFF_WB + 3 * 8192
OFF_SQ = OFF_HB + 16 * G
OFF_ST = OFF_SQ + 4 * G
OFF_UT = OFF_ST + 4 * T32
OFF_BIG = OFF_UT + 1024
BIG_SZ = 50 * 512
AR_TOTAL = OFF_BIG + BIG_SZ


class P2:
    def __init__(self, S, A, PS, dram, vecs, consts, NG, do_l0, do_l1):
        self.S, self.A, self.PS, self.d, self.NG = S, A, PS, dram, NG
        self.do_l0, self.do_l1 = do_l0, do_l1
        self.VEC = vecs
        self.ONES, self.IDENT, self.ONESF, self.SM = consts
        self.H = Reg(A, OFF_H, G, F32)
        self.HB = Reg(A, OFF_HB, G)
        self.SQ = Reg(A, OFF_SQ, G)
        self.ST = Reg(A, OFF_ST, G, F32)
        self.BIG = Reg(A, OFF_BIG, G)
        self.KT = [Reg(A, OFF_KV + l * 8192, 256) for l in range(2)]
        self.VM = [Reg(A, OFF_KV + l * 8192 + 4096, 2048) for l in range(2)]
        self.WS = WStream(S, A, OFF_WB, 3)
        self.sqi = 0
        self.out_dmas = []

    def vcol(self, c, n=1):
        return V(self.VEC.h[:, c:c + n], self.VEC.tks)

    def layer_norm(self, gcol, bcol, src=None, dst32=None, dstb=None, silu=False):
        S, PS = self.S, self.PS
        src = src or self.H
        ps1, ps2 = PS.get(), PS.get()
        for c in range(KC):
            zb = self.SQ.t(self.sqi % 4); z2 = self.SQ.t((self.sqi + 1) % 4); self.sqi += 2
            S.act(zb, src.t(c), AF.Copy)
            S.act(z2, src.t(c), AF.Square)
            S.mm(ps1, self.ONES, zb, c == 0, c == KC - 1)
            S.mm(ps2, self.ONES, z2, c == 0, c == KC - 1)
        mean, msq, rstd = self.ST.t(0), self.ST.t(1), self.ST.t(2)
        S.ts("dve", mean, ps1, 1.0 / D, None, ALU.mult)
        S.tt("dve", msq, mean, mean, ALU.mult)
        S.stt("dve", rstd, ps2, 1.0 / D, msq, ALU.mult, ALU.subtract)
        S.ts("dve", rstd, rstd, 1.0, LN_EPS, ALU.mult, ALU.add)
        S.act(rstd, rstd, AF.Sqrt)
        S.op("dve", lambda e: e.reciprocal(out=rstd.ap, in_=rstd.ap), reads=[rstd], writes=[rstd])
        for c in range(KC):
            x = src.t(c)
            o32 = dst32.t(c) if dst32 is not None else x
            S.tt("dve", x, x, mean, ALU.subtract)
            S.tt("dve", x, x, rstd, ALU.mult)
            if silu:
                S.act(dstb.t(c), x, AF.Silu, scale=self.vcol(gcol + c), bias=self.vcol(bcol + c))
            else:
                S.act(o32, x, AF.Identity, scale=self.vcol(gcol + c), bias=self.vcol(bcol + c))
                S.act((dstb or self.HB).t(c), o32, AF.Copy)

    def linear(self, w2d, nk, col0, noc, rhs_fn, evac, n=G, cb=512):
        S, PS = self.S, self.PS
        per = cb // 128
        for b0 in range(0, noc, per):
            nb = min(per, noc - b0)
            blk = self.WS.load(w2d, 0, nk, col0 + b0 * 128, nb * 128)
            for j in range(nb):
                ps = PS.get(n)
                for k in range(nk):
                    S.mm(ps, blk(k, j * 128, (j + 1) * 128), rhs_fn(k), k == 0, k == nk - 1)
                evac(b0 + j, ps)

    def resid_evac(self, oc, ps):
        h = self.H.t(oc)
        self.S.stt("dve", h, h, ALPHA, ps, ALU.mult, ALU.add)

    def mem_kv(self, l, MB):
        S, PS = self.S, self.PS
        KT, VM = self.KT[l], self.VM[l]
        self.linear(self.d["xattn_wk"][l], KC, 0, KC, lambda k: MB.t(k),
                    lambda oc, ps: S.act(KT.t(oc), ps, AF.Copy), n=256)
        wv = self.d["xattn_wv"][l]
        for cbk in range(4):
            blk = self.WS.load(wv, 0, KC, cbk * 512, 512)
            for mt in range(2):
                ps = PS.get()
                for k in range(KC):
                    S.mm(ps, MB.t(k, mt * 128, (mt + 1) * 128), blk(k, 0, 512), k == 0, k == KC - 1)
                S.act(VM.t(mt, cbk * 512, (cbk + 1) * 512), ps, AF.Copy)

    def xattn(self, l):
        S, PS, BIG = self.S, self.PS, self.BIG
        KT, VM = self.KT[l], self.VM[l]
        self.linear(self.d["xattn_wq"][l], KC, 0, KC, lambda k: self.HB.t(k),
                    lambda oc, ps: S.act(BIG.t(oc), ps, AF.Copy))
        for hd in range(4):
            pts = []
            for mt in range(2):
                ps = PS.get()
                for c in range(4):
                    S.mm(ps, KT.t(hd * 4 + c, mt * 128, (mt + 1) * 128), BIG.t(hd * 4 + c), c == 0, c == 3)
                pt = BIG.t(32 + (hd % 2) * 2 + mt)
                S.act(pt, ps, AF.Exp, scale=XSCALE)
                pts.append(pt)
            psl = PS.get()
            for mt in range(2):
                S.mm(psl, self.ONES, pts[mt], mt == 0, mt == 1)
            rl = self.ST.t(3)
            S.op("dve", lambda e, rl=rl, psl=psl: e.reciprocal(out=rl.ap, in_=psl.ap), reads=[psl], writes=[rl])
            for c in range(4):
                ps = PS.get()
                for mt in range(2):
                    S.mm(ps, VM.t(mt, (hd * 4 + c) * 128, (hd * 4 + c + 1) * 128), pts[mt], mt == 0, mt == 1)
                S.tt("dve", BIG.t(16 + hd * 4 + c), ps, rl, ALU.mult)
        self.linear(self.d["xattn_wo"][l], KC, 0, KC, lambda k: BIG.t(16 + k), self.resid_evac)
        self.layer_norm(VO["lng", l, 1], VO["lnb", l, 1])

    def ffn(self, l):
        S, PS, BIG = self.S, self.PS, self.BIG
        w1 = self.d["ffn_w_in"][l]
        for b in range(FC // 4):
            blk1 = self.WS.load(w1, 0, KC, b * 512, 512)
            blk2 = self.WS.load(w1, 0, KC, DFF + b * 512, 512)
            for j in range(4):
                p1, p2 = PS.get(), PS.get()
                for k in range(KC):
                    S.mm(p1, blk1(k, j * 128, (j + 1) * 128), self.HB.t(k), k == 0, k == KC - 1)
                for k in range(KC):
                    S.mm(p2, blk2(k, j * 128, (j + 1) * 128), self.HB.t(k), k == 0, k == KC - 1)
                sl = self.ST.t(2 + (j % 2))
                S.act(sl, p1, AF.Silu)
                S.tt("dve", BIG.t(b * 4 + j), sl, p2, ALU.mult)
        self.linear(self.d["ffn_w_out"][l], FC, 0, KC, lambda k: BIG.t(k), self.resid_evac, cb=128)
        self.layer_norm(VO["lng", l, 2], VO["lnb", l, 2])

    def mix_tail(self, t0):
        S, PS, BIG, H = self.S, self.PS, self.BIG, self.H
        d = self.d
        fmv = lambda ap, n: ap.rearrange("(k p) t -> p k t", p=128)[:, 0:n, t0:t0 + G]
        S.dma("sp", H.multi(0, 8).ap, fmv(d["o0T"], 8), writes=[H.multi(0, 8)])
        S.dma("sp", H.multi(8, 8).ap, fmv(d["o1T"], 8), writes=[H.multi(8, 8)])
        S.dma("pool", BIG.multi(0, 8).ap, fmv(d["retT"], 8), writes=[BIG.multi(0, 8)])
        neglam = V(self.SM.h[:, 0:1], self.SM.tks)
        for hd in range(4):
            pss = PS.get()
            for j in range(2):
                c = hd * 2 + j
                S.stt("dve", H.t(c), H.t(8 + c), neglam, H.t(c), ALU.mult, ALU.add)
                z2 = self.SQ.t(self.sqi % 4); self.sqi += 1
                S.act(z2, H.t(c), AF.Square)
                S.mm(pss, self.ONES, z2, j == 0, j == 1)
            rstd = self.ST.t(hd % 2)
            S.ts("dve", rstd, pss, 1.0 / 256, LN_EPS, ALU.mult, ALU.add)
            S.act(rstd, rstd, AF.Sqrt)
            S.op("dve", lambda e, r=rstd: e.reciprocal(out=r.ap, in_=r.ap), reads=[rstd], writes=[rstd])
            for j in range(2):
                c = hd * 2 + j
                S.stt("dve", BIG.t(8 + c), H.t(c), self.vcol(VO["sub_g"] + c), rstd, ALU.mult, ALU.mult)
        S.dma("sp", H.multi(0, 16).ap, fmv(d["xT"], 16), writes=[H.multi(0, 16)])
        self.linear(d["w_mix_out"], KC, 0, KC, lambda k: BIG.t(k), self.resid_evac)
        self.layer_norm(VO["lng", 0, 0], VO["lnb", 0, 0])

    def conv_module(self, g, t0):
        S, PS, BIG, H, HB = self.S, self.PS, self.BIG, self.H, self.HB
        d = self.d
        UW = 32 + G
        UB = Reg(self.A, OFF_BIG, UW)
        DG = Reg(self.A, OFF_BIG + 8960, 128, stride=256)
        Y2 = Reg(self.A, OFF_BIG + 8960 + 31 * 256, G, F32)
        UT = Reg(self.A, OFF_UT, 32)
        HH = Reg(self.A, OFF_UT + 512, 32)
        w1 = d["conv_w_pw1"]
        bcol = VO["b_pw1"]
        hmask = self.vcol(VO["hmask"])
        for b in range(4):
            blka = self.WS.load(w1, 0, KC, b * 512, 512)
            blkg = self.WS.load(w1, 0, KC, D + b * 512, 512)
            for j in range(4):
                oc = b * 4 + j
                pa, pg = PS.get(), PS.get()
                for k in range(KC):
                    S.mm(pa, blka(k, j * 128, (j + 1) * 128), HB.t(k), k == 0, k == KC - 1)
                for k in range(KC):
                    S.mm(pg, blkg(k, j * 128, (j + 1) * 128), HB.t(k), k == 0, k == KC - 1)
                sg = self.ST.t(2 + (j % 2))
                S.act(sg, pg, AF.Sigmoid, bias=self.vcol(bcol + 16 + oc))
                S.stt("dve", UB.t(oc, 32, UW), pa, self.vcol(bcol + oc), sg, ALU.add, ALU.mult)
                if g == 0:
                    pa2, pg2 = PS.get(32), PS.get(32)
                    for k in range(KC):
                        S.mm(pa2, blka(k, j * 128, (j + 1) * 128), HH.t(k), k == 0, k == KC - 1)
                    for k in range(KC):
                        S.mm(pg2, blkg(k, j * 128, (j + 1) * 128), HH.t(k), k == 0, k == KC - 1)
                    sg2 = self.ST.t(3, 0, 32)
                    S.act(sg2, pg2, AF.Sigmoid, bias=self.vcol(bcol + 16 + oc))
                    S.stt("dve", sg2, pa2, self.vcol(bcol + oc), sg2, ALU.add, ALU.mult)
                    S.ts("dve", UB.t(oc, 0, 32), sg2, hmask, None, ALU.mult)
                else:
                    S.copy("dve", UB.t(oc, 0, 32), UT.t(oc))
        Y1 = Reg(self.A, OFF_HB, G, F32)
        ytile = lambda c: Y1.t(c) if c < 8 else Y2.t(c - 8)
        for c in range(KC):
            ps = PS.get()
            for k in range(31):
                dg = DG.t(k)
                S.ts("dve", dg, self.IDENT, self.vcol(VO["w_dw"] + k * 16 + c), None, ALU.mult)
                S.mm(ps, dg, UB.t(c, 2 + k, 2 + k + G), k == 0, k == 30)
            S.act(ytile(c), ps, AF.Identity, bias=self.vcol(VO["b_dw"] + c))
            S.copy("dve", UT.t(c), UB.t(c, G, UW))

        class _Y:
            t = staticmethod(lambda c, *a: ytile(c))
        CN = Reg(self.A, OFF_BIG, G)
        self.layer_norm(VO["cln_g"], VO["cln_b"], src=_Y, dstb=CN, silu=True)
        def evac(oc, ps):
            h = H.t(oc)
            tmp = self.ST.t(2 + (oc % 2))
            S.act(tmp, ps, AF.Identity, bias=self.vcol(VO["b_pw2"] + oc))
            S.stt("dve", h, h, ALPHA, tmp, ALU.mult, ALU.add)
        self.linear(d["conv_w_pw2"], KC, 0, KC, lambda k: CN.t(k), evac)
        self.layer_norm(VO["lng", 1, 0], VO["lnb", 1, 0])

    def prologue(self):
        S, PS = self.S, self.PS
        S.memset("dve", self.ONES, 1.0)
        S.memset("dve", self.ONESF, 1.0)
        S.dma("pool", self.IDENT.ap, self.d["ident"], writes=[self.IDENT])
        S.dma("sp", self.VEC.h[:], self.d["vecs"], writes=[self.VEC.all()])
        if self.do_l0:
            pr = V(self.SM.h[:, 2:4], self.SM.tks)
            S.tt("dve", V(self.SM.h[:, 2:3], self.SM.tks), self.vcol(VO["lam"]), self.vcol(VO["lam"] + 1), ALU.mult)
            S.tt("dve", V(self.SM.h[:, 3:4], self.SM.tks), self.vcol(VO["lam"] + 2), self.vcol(VO["lam"] + 3), ALU.mult)
            ps = PS.get(2)
            S.mm(ps, self.ONESF, pr, True, True)
            ex = V(self.SM.h[:, 4:6], self.SM.tks)
            S.act(ex, ps, AF.Exp)
            nl = V(self.SM.h[:, 0:1], self.SM.tks)
            S.tt("dve", nl, V(self.SM.h[:, 5:6], self.SM.tks), V(self.SM.h[:, 4:5], self.SM.tks), ALU.subtract)
            S.ts("dve", nl, nl, -LAMBDA_INIT, None, ALU.add)
            sg = self.vcol(VO["sub_g"], 8)
            S.ts("dve", sg, sg, 1.0 - LAMBDA_INIT, None, ALU.mult)
        MB = Reg(self.A, OFF_BIG, 256)
        mbv = MB.multi(0, 16)
        S.dma("pool", mbv.ap, self.d["memT"].rearrange("(k p) m -> p k m", p=128), writes=[mbv])
        for l in ([0] if self.do_l0 else []) + ([1] if self.do_l1 else []):
            self.mem_kv(l, MB)

    def run(self):
        S, H, d = self.S, self.H, self.d
        self.prologue()
        for g in range(self.NG):
            t0 = g * G
            fmv = lambda ap: ap.rearrange("(k p) t -> p k t", p=128)[:, :, t0:t0 + G]
            if self.do_l0:
                self.mix_tail(t0)
                self.xattn(0)
                self.ffn(0)
            if self.do_l1:
                if not self.do_l0:
                    S.dma("sp", H.multi(0, 16).ap, fmv(d["h1T"]), writes=[H.multi(0, 16)])
                    for c in range(KC):
                        S.act(self.HB.t(c), H.t(c), AF.Copy)
                if g == 0:
                    HH = Reg(self.A, OFF_UT + 512, 32)
                    hv = HH.multi(0, 16)
                    S.dma("pool", hv.ap, d["haloT"].rearrange("(k p) t -> p k t", p=128), writes=[hv])
                self.conv_module(g, t0)
                self.xattn(1)
                self.ffn(1)
            outname = "outT" if self.do_l1 else "h1T_out"
            self.out_dmas.append(S.dma("sp", fmv(d[outname]), H.multi(0, 16).ap, reads=[H.multi(0, 16)]))


def build_p2(NG, do_l0, do_l1):
    nc = bass.Bass("TRN2", target_bir_lowering=False)
    NT = NG * G
    d = {}

    def inp(name, shape):
        d[name] = nc.dram_tensor(name, shape, F32, kind="ExternalInput").ap()

    inp("vecs", [128, NV]); inp("ident", [128, 128]); inp("memT", [D, NMEM])
    for n in ("xattn_wq", "xattn_wk", "xattn_wv", "xattn_wo"):
        inp(n, [2, D, D])
    inp("ffn_w_in", [2, D, 2 * DFF]); inp("ffn_w_out", [2, DFF, D])
    if do_l0:
        inp("xT", [D, NT]); inp("retT", [1024, NT]); inp("o0T", [1024, NT]); inp("o1T", [1024, NT])
        inp("w_mix_out", [D, D])
    if do_l1:
        inp("conv_w_pw1", [D, 2 * D]); inp("conv_w_pw2", [D, D]); inp("haloT", [D, 32])
        if not do_l0:
            inp("h1T", [D, NT])
    outname = "outT" if do_l1 else "h1T_out"
    d[outname] = nc.dram_tensor(outname, [D, NT], F32, kind="ExternalOutput").ap()
    with ExitStack() as es:
        ar = es.enter_context(nc.sbuf_tensor("ar", [128, AR_TOTAL], BF16))
        vec = es.enter_context(nc.sbuf_tensor("vec", [128, NV], F32))
        ones = es.enter_context(nc.sbuf_tensor("ones", [128, 128], BF16))
        ident = es.enter_context(nc.sbuf_tensor("identb", [128, 128], BF16))
        onesf = es.enter_context(nc.sbuf_tensor("onesf", [128, 128], F32))
        sm = es.enter_context(nc.sbuf_tensor("sm", [128, 8], F32))
        ps = es.enter_context(nc.psum_tensor("ps", [128, 8, 512], F32))
        S = Sched(nc, es)
        A = Arena(ar, AR_TOTAL)
        consts = (V(ones[:], [Tk("ones")]), V(ident[:], [Tk("ident")]), V(onesf[:], [Tk("onesf")]), Buf(sm, 1, "sm"))
        p2 = P2(S, A, PsumPool(ps), d, Buf(vec, 1, "vec"), consts, NG, do_l0, do_l1)
        p2.run()
        S.emit(p2.out_dmas)
    return nc


TB_OFF, TB_DIAG, TB_CT, TB_DM, TB_XI, TB_SC = 0, 512, 2560, 2688, 2816, 2944
NTAB = 2948


def p1_tables(core, inp):
    h = core
    gamma = 1.0 - 2.0 ** (-5.0 - h)
    lg = np.log1p(-(2.0 ** (-5.0 - h)))
    slope = 2.0 ** (-8.0 * (core // 2 + 1) / 4)
    tab = np.zeros((128, NTAB), np.float64)
    p = np.arange(128)[:, None]
    f = np.arange(512)[None, :]
    tab[:, TB_OFF:TB_OFF + 512] = -slope * (f - p)
    for r in range(4):
        dist = f - p - 128 * r
        tab[:, TB_DIAG + r * 512:TB_DIAG + (r + 1) * 512] = np.where(dist >= 0, -slope * dist, -30000.0)
    tab[:, TB_CT:TB_CT + 128] = -slope * 128.0 * np.arange(128)[None, :]
    i = np.arange(128)[None, :]
    tab[:, TB_DM:TB_DM + 128] = np.where(i - p >= 0, np.exp(lg * np.maximum(i - p, 0)), 0.0) * (128 ** -0.5)
    tab[:, TB_XI:TB_XI + 128] = np.exp(lg * (i + 1.0))
    tab[:, TB_SC + 0] = np.exp(lg * (127.0 - np.arange(128))) * (128 ** -0.5)
    tab[:, TB_SC + 1] = np.exp(lg * 128.0)
    tab[:, TB_SC + 2] = np.asarray(inp["ret_gn_g"][0, h * 128:(h + 1) * 128], np.float64)
    tab[:, TB_SC + 3] = np.asarray(inp["ret_gn_b"][0, h * 128:(h + 1) * 128], np.float64)
    return tab.astype(np.float32)


def p1_weights(core, w_in):
    h, hd, m = core, core // 2, core % 2
    sl = lambda a: w_in[:, a:a + 128]
    wf = np.concatenate([sl(h * 128), sl(1024 + h * 128), sl(3072 + h * 128),
                         sl(4096 + hd * 256 + m * 128), sl(5120 + hd * 256 + m * 128)], 1)
    wt = np.concatenate([sl(1024 + h * 128), sl(2048 + h * 128), w_in[:, 6144 + hd * 256:6144 + (hd + 1) * 256]], 1)
    return np.ascontiguousarray(wf), np.ascontiguousarray(wt)


class P1:
    parts = ('tm', 'ev1', 'ev2', 'ev3', 'ret', 'gn', 'gn2', 'att', 'att2')

    def __init__(self, S, A, PS, d, tab, ones, NTOK):
        self.S, self.A, self.PS, self.d, self.TAB, self.ONES, self.NTOK = S, A, PS, d, tab, ones, NTOK
        self.out_dmas = []

    def tb(self, c, n=1):
        return V(self.TAB.h[:, c:c + n], self.TAB.tks)

    def run(self):
        S, A, PS, d, NTOK = self.S, self.A, self.PS, self.d, self.NTOK
        o = 0
        def reg(n, width, dt=BF16, stride=None):
            nonlocal o
            o = -(-o // 256) * 256
            r = Reg(A, o, width, dt, stride)
            o += n * (stride or width) * (2 if dt == F32 else 1)
            return r
        DKT = reg(1, NTOK)
        DV = reg(NTOK // 128, 256)
        WF = reg(16, 640)
        WT = reg(16, 512)
        XB = reg(16, 512)
        RQ, RKF, DQ = reg(1, 512), reg(1, 512), reg(1, 512)
        RG, YR = reg(1, 512, F32), reg(1, 512, F32)
        RKZ, TM = reg(4, 128, stride=256), reg(4, 512)
        SMT, QX = reg(2, 128, stride=256), reg(2, 128, stride=256)
        STATE = reg(1, 128, F32)
        STATEB = reg(1, 128)
        NTT = 5
        TT = reg(NTT, 512, F32)
        PP = reg(NTT, 512)
        YB = reg(2, 512)
        ST = reg(3, 512, F32)
        OO = reg(2, 512, F32)
        assert o <= 3 * NTOK + 50176, o
        PS.nrot = 5
        po0, po1, pl = PS.v(5, 0, 512), PS.v(6, 0, 512), PS.v(7, 0, 512)

        S.memset("dve", self.ONES, 1.0)
        S.dma("sp", self.TAB.h[:], d["tab"], writes=[self.TAB.all()])
        S.dma("pool", WF.multi(0, 16).ap, d["wf"].rearrange("(k p) c -> p k c", p=128), writes=[WF.multi(0, 16)])
        S.dma("pool", WT.multi(0, 16).ap, d["wt"].rearrange("(k p) c -> p k c", p=128), writes=[WT.multi(0, 16)])
        S.memset("dve", STATE.t(0), 0.0)
        S.memset("dve", STATEB.t(0), 0.0)
        QS = 128 ** -0.5
        for g in range(NTOK // 512):
            t0 = g * 512
            xv = XB.multi(0, 16)
            S.dma("pool", xv.ap, d["xT"].rearrange("(k p) t -> p k t", p=128)[:, :, t0:t0 + 512], writes=[xv])
            for j in range(5):
                ps = PS.get()
                for k in range(16):
                    S.mm(ps, WF.t(k, j * 128, (j + 1) * 128), XB.t(k), k == 0, k == 15)
                if j == 0:
                    S.act(RQ.t(0), ps, AF.Copy)
                elif j == 1:
                    S.act(RKF.t(0), ps, AF.Copy)
                elif j == 2:
                    S.act(RG.t(0), ps, AF.Silu)
                elif j == 3:
                    S.act(DQ.t(0), ps, AF.Copy, scale=QS)
                else:
                    S.act(DKT.t(0, t0, t0 + 512), ps, AF.Copy)
            for tt in (range(4) if 'tm' in self.parts else []):
                ps = PS.get()
                k_ = PS.i - 1
                for k in range(16):
                    S.mm(ps, XB.t(k, tt * 128, (tt + 1) * 128), WT.t(k), k == 0, k == 15)
                kk = k_ % PS.nrot
                S.act(TM.t(tt), ps, AF.Copy)
                S.ts("dve", RKZ.t(tt), TM.t(tt, 0, 128), self.tb(TB_SC + 0), None, ALU.mult)
                S.copy("dve", DV.t(4 * g + tt), TM.t(tt, 256, 512))
            for tt in (range(4) if 'ret' in self.parts else []):
                a, b = tt * 128, (tt + 1) * 128
                pss = PS.get(128)
                S.mm(pss, RKF.t(0, a, b), RQ.t(0, a, b), True, True)
                sm = SMT.t(tt % 2); qx = QX.t(tt % 2)
                S.tt("dve", sm, pss, self.tb(TB_DM, 128), ALU.mult)
                S.tt("dve", qx, RQ.t(0, a, b), self.tb(TB_XI, 128), ALU.mult)
                psy = PS.get(128)
                S.mm(psy, TM.t(tt, 128, 256), sm, True, False)
                S.mm(psy, STATEB.t(0), qx, False, True)
                S.act(YR.t(0, a, b), psy, AF.Copy)
                pkv = PS.get(128)
                S.mm(pkv, RKZ.t(tt), TM.t(tt, 128, 256), True, True)
                S.stt("dve", STATE.t(0), STATE.t(0), self.tb(TB_SC + 1), pkv, ALU.mult, ALU.add)
                S.act(STATEB.t(0), STATE.t(0), AF.Copy)
            y = YR.t(0)
            if 'ret' not in self.parts:
                S.memset('dve', y, 0.0)
            S.act(YB.t(0), y, AF.Copy)
            S.act(YB.t(1), y, AF.Square)
            ps1, ps2 = PS.get(), PS.get()
            S.mm(ps1, self.ONES, YB.t(0), True, True)
            S.mm(ps2, self.ONES, YB.t(1), True, True)
            mean, msq, rstd = ST.t(0), ST.t(1), ST.t(2)
            S.ts("dve", mean, ps1, 1.0 / 128, None, ALU.mult)
            if 'gn' not in self.parts:
                S.memset('dve', rstd, 1.0)
            S.tt("dve", msq, mean, mean, ALU.mult)
            S.stt("dve", rstd, ps2, 1.0 / 128, msq, ALU.mult, ALU.subtract)
            S.ts("dve", rstd, rstd, 1.0, LN_EPS, ALU.mult, ALU.add)
            S.act(rstd, rstd, AF.Sqrt)
            S.op("dve", lambda e, r=rstd: e.reciprocal(out=r.ap, in_=r.ap), reads=[rstd], writes=[rstd])
            S.tt("dve", y, y, mean, ALU.subtract)
            S.tt("dve", y, y, rstd, ALU.mult)
            S.act(y, y, AF.Identity, scale=self.tb(TB_SC + 2), bias=self.tb(TB_SC + 3))
            S.tt("dve", y, y, RG.t(0), ALU.mult)
            self.out_dmas.append(S.dma("sp", d["ret_out"][:, t0:t0 + 512], y.ap, reads=[y]))
            nj = 4 * g + 4
            if 'att2' not in self.parts:
                nj = 1
            if 'att' not in self.parts:
                nj = 0
            AHEAD = 3
            pend = {}

            def score(j):
                ps = PS.get()
                S.mm(ps, DKT.t(0, j * 128, (j + 1) * 128), DQ.t(0), True, True)
                r = j - 4 * g
                bt = self.tb(TB_OFF, 512) if r < 0 else self.tb(TB_DIAG + r * 512, 512)
                t = TT.t(j % NTT); p = PP.t(j % NTT)
                S.tt("dve", t, ps, bt, ALU.add)
                if r < 0:
                    S.act(p, t, AF.Exp, bias=self.tb(TB_CT + (4 * g - j)))
                else:
                    S.act(p, t, AF.Exp)
                pend[j] = p

            for j in range(min(AHEAD, nj)):
                score(j)
            for j in range(nj):
                if j + AHEAD < nj:
                    score(j + AHEAD)
                p = pend.pop(j)
                S.mm(po0, DV.t(j, 0, 128), p, j == 0, j == nj - 1)
                S.mm(po1, DV.t(j, 128, 256), p, j == 0, j == nj - 1)
                S.mm(pl, self.ONES, p, j == 0, j == nj - 1)
            rl = ST.t(0)
            S.op("dve", lambda e, rl=rl: e.reciprocal(out=rl.ap, in_=pl.ap), reads=[pl], writes=[rl])
            S.tt("dve", OO.t(0), po0, rl, ALU.mult)
            S.tt("dve", OO.t(1), po1, rl, ALU.mult)
            ov = OO.multi(0, 2)
            self.out_dmas.append(S.dma("sp", d["o_out"].rearrange("(k p) t -> p k t", p=128)[:, :, t0:t0 + 512], ov.ap, reads=[ov]))


def build_p1(NTOK):
    nc = bass.Bass("TRN2", target_bir_lowering=False)
    d = {}
    for name, shape in (("xT", [D, NTOK]), ("wf", [D, 640]), ("wt", [D, 512]), ("tab", [128, NTAB])):
        d[name] = nc.dram_tensor(name, shape, F32, kind="ExternalInput").ap()
    p1o = nc.dram_tensor("p1_out", [384, NTOK], F32, kind="ExternalOutput").ap()
    d["ret_out"] = p1o[0:128, :]
    d["o_out"] = p1o[128:384, :]
    ar_total = 3 * NTOK + 50176
    with ExitStack() as es:
        ar = es.enter_context(nc.sbuf_tensor("ar", [128, ar_total], BF16))
        tab = es.enter_context(nc.sbuf_tensor("tabs", [128, NTAB], F32))
        ones = es.enter_context(nc.sbuf_tensor("ones", [128, 128], BF16))
        ps = es.enter_context(nc.psum_tensor("ps", [128, 8, 512], F32))
        S = Sched(nc, es)
        A = Arena(ar, ar_total)
        p1 = P1(S, A, PsumPool(ps), d, Buf(tab, 1, "tab"), V(ones[:], [Tk("ones")]), NTOK)
        p1.run()
        S.emit(p1.out_dmas)
    return nc


_NC_CACHE = {}


def _get(key, fn):
    if key not in _NC_CACHE:
        _NC_CACHE[key] = fn()
    return _NC_CACHE[key]


def kernel(**inp):
    inp = {k: np.asarray(v) for k, v in inp.items()}
    cores = list(range(NCORES))
    TPC = SEQ // NCORES
    NG = TPC // G
    xT = np.ascontiguousarray(inp["x"][0].T)
    memT = np.ascontiguousarray(inp["mem"][0].T)
    ident = np.eye(128, dtype=np.float32)

    nc1 = _get("p1", lambda: build_p1(SEQ))
    maps = []
    for c in cores:
        wf, wt = p1_weights(c, inp["w_in"][0])
        maps.append({"xT": xT, "wf": wf, "wt": wt, "tab": p1_tables(c, inp)})
    r1 = run_bass_kernel_spmd(nc1, maps, core_ids=cores).results
    retT = np.concatenate([r1[c]["p1_out"][0:128] for c in cores], 0)
    o0T = np.concatenate([r1[2 * h]["p1_out"][128:384] for h in range(4)], 0)
    o1T = np.concatenate([r1[2 * h + 1]["p1_out"][128:384] for h in range(4)], 0)
    del r1

    common = {"ident": ident, "memT": memT}
    for n in ("xattn_wq", "xattn_wk", "xattn_wv", "xattn_wo", "ffn_w_in", "ffn_w_out"):
        common[n] = inp[n]

    nc2 = _get("p2a", lambda: build_p2(NG, True, False))
    maps = []
    for c in cores:
        sl = slice(c * TPC, (c + 1) * TPC)
        m = dict(common)
        m.update(vecs=pack_vecs(inp, 1.0), xT=np.ascontiguousarray(xT[:, sl]), retT=np.ascontiguousarray(retT[:, sl]),
                 o0T=np.ascontiguousarray(o0T[:, sl]), o1T=np.ascontiguousarray(o1T[:, sl]), w_mix_out=inp["w_mix_out"][0])
        maps.append(m)
    r2 = run_bass_kernel_spmd(nc2, maps, core_ids=cores).results
    h1 = [r2[c]["h1T_out"] for c in cores]
    del r2

    nc3 = _get("p2b", lambda: build_p2(NG, False, True))
    maps = []
    for c in cores:
        m = dict(common)
        halo = h1[c - 1][:, TPC - HALO:] if c > 0 else np.zeros((D, HALO), np.float32)
        m.update(vecs=pack_vecs(inp, 1.0 if c > 0 else 0.0), h1T=h1[c], haloT=np.ascontiguousarray(halo),
                 conv_w_pw1=inp["conv_w_pw1"][0], conv_w_pw2=inp["conv_w_pw2"][0])
        maps.append(m)
    r3 = run_bass_kernel_spmd(nc3, maps, core_ids=cores).results
    outT = np.concatenate([r3[c]["outT"] for c in cores], 1)
    return np.ascontiguousarray(outT.T)[None].astype(np.float32)
```

```python
import math
from contextlib import ExitStack

import numpy as np
import concourse.bass as bass
import concourse.mybir as mybir
from concourse.bass_utils import run_bass_kernel_spmd

F32 = mybir.dt.float32
BF16 = mybir.dt.bfloat16
AF = mybir.ActivationFunctionType
ALU = mybir.AluOpType

D = 2048
KC = 16
SEQ = 16384
NCORES = 8
NMEM = 256
DFF = 5632
FC = DFF // 128
ALPHA = (2.0 * 2) ** 0.25
LN_EPS = 1e-5
LAMBDA_INIT = 0.8 - 0.6 * math.exp(-0.3 * 1)
XSCALE = 512 ** -0.5
HALO = 32


class Tk:
    __slots__ = ("w", "r", "name")

    def __init__(self, name=""):
        self.w = None
        self.r = []
        self.name = name


class V:
    __slots__ = ("ap", "tks")

    def __init__(self, ap, tks):
        self.ap = ap
        self.tks = tks


class Buf:
    def __init__(self, handle, nslots, name):
        self.h = handle
        self.n = nslots
        self.tks = [Tk(f"{name}[{i}]") for i in range(nslots)]

    def s(self, i, *idx):
        if idx:
            return V(self.h[(slice(None), i) + idx], [self.tks[i]])
        return V(self.h[:, i], [self.tks[i]])

    def rng(self, i0, i1, *idx):
        return V(self.h[(slice(None), slice(i0, i1)) + idx], self.tks[i0:i1])

    def all(self):
        return V(self.h[:], self.tks)


class Op:
    __slots__ = ("eng", "fn", "deps", "signal", "val", "is_dma", "sem", "inc")

    def __init__(self, eng, fn, is_dma):
        self.eng = eng
        self.fn = fn
        self.deps = []
        self.signal = False
        self.val = None
        self.is_dma = is_dma
        self.sem = None
        self.inc = 1


ENGS = ("pe", "act", "dve", "pool", "sp")
NDMA = {"sp": 12, "pool": 12, "act": 4}


class Sched:
    def __init__(self, nc, es):
        self.nc = nc
        self.ops = {e: [] for e in ENGS}
        self.esem = {e: es.enter_context(nc.semaphore(f"e_{e}")) for e in ENGS}
        self.dsem = {q: [es.enter_context(nc.semaphore(f"d_{q}{i}")) for i in range(n)]
                     for q, n in NDMA.items()}
        self.dcnt = {q: 0 for q in NDMA}
        self.dlast = {q: [None] * n for q, n in NDMA.items()}

    def _add(self, op, reads, writes):
        deps = {}

        def need(d, raw):
            if d is None or d is op:
                return
            if d.is_dma or op.is_dma:
                deps[id(d)] = d
            elif d.eng == op.eng:
                if raw and op.eng != "pe":
                    deps[id(d)] = d
            else:
                deps[id(d)] = d

        for v in reads:
            for tk in v.tks:
                need(tk.w, True)
        for v in writes:
            for tk in v.tks:
                need(tk.w, False)
                for r in tk.r:
                    need(r, False)
        for v in reads:
            for tk in v.tks:
                tk.r.append(op)
        for v in writes:
            for tk in v.tks:
                tk.w = op
                tk.r = []
        for d in deps.values():
            d.signal = True
            op.deps.append(d)
        self.ops[op.eng].append(op)
        return op

    def op(self, eng, fn, reads=(), writes=()):
        return self._add(Op(eng, fn, False), reads, writes)

    def dma(self, q, out_ap, in_ap, reads=(), writes=()):
        op = Op(q, lambda e: e.dma_start(out=out_ap, in_=in_ap), True)
        n = self.dcnt[q]
        self.dcnt[q] += 1
        k = n % NDMA[q]
        op.sem = self.dsem[q][k]
        op.val = 16 * (n // NDMA[q] + 1)
        op.inc = 16
        op.signal = True
        prev = self.dlast[q][k]
        self._add(op, reads, writes)
        if prev is not None:
            op.deps.append(prev)
        self.dlast[q][k] = op
        return op

    def mm(self, out, lhsT, rhs, start, stop):
        return self.op("pe", lambda e: e.matmul(out.ap, lhsT=lhsT.ap, rhs=rhs.ap, start=start, stop=stop),
                       reads=[lhsT, rhs], writes=[out])

    def act(self, out, in_, func, bias=None, scale=None, extra_reads=(), eng="act"):
        kw = {}
        if bias is not None:
            kw["bias"] = bias.ap if isinstance(bias, V) else bias
        if scale is not None:
            kw["scale"] = scale.ap if isinstance(scale, V) else scale
        rd = [in_] + [x for x in (bias, scale) if isinstance(x, V)] + list(extra_reads)
        return self.op(eng, lambda e: e.activation(out=out.ap, in_=in_.ap, func=func, **kw),
                       reads=rd, writes=[out])

    def tt(self, eng, out, in0, in1, op):
        return self.op(eng, lambda e: e.tensor_tensor(out=out.ap, in0=in0.ap, in1=in1.ap, op=op),
                       reads=[in0, in1], writes=[out])

    def ts(self, eng, out, in0, s1, s2, op0, op1=None):
        rd = [in0] + [x for x in (s1, s2) if isinstance(x, V)]
        a1 = s1.ap if isinstance(s1, V) else s1
        a2 = s2.ap if isinstance(s2, V) else s2
        if op1 is None:
            return self.op(eng, lambda e: e.tensor_scalar(out=out.ap, in0=in0.ap, scalar1=a1, scalar2=None, op0=op0),
                           reads=rd, writes=[out])
        return self.op(eng, lambda e: e.tensor_scalar(out=out.ap, in0=in0.ap, scalar1=a1, scalar2=a2, op0=op0, op1=op1),
                       reads=rd, writes=[out])

    def stt(self, eng, out, in0, scalar, in1, op0, op1):
        rd = [in0, in1] + ([scalar] if isinstance(scalar, V) else [])
        a = scalar.ap if isinstance(scalar, V) else scalar
        return self.op(eng, lambda e: e.scalar_tensor_tensor(out=out.ap, in0=in0.ap, scalar=a, in1=in1.ap, op0=op0, op1=op1),
                       reads=rd, writes=[out])

    def copy(self, eng, out, in_):
        return self.op(eng, lambda e: e.tensor_copy(out=out.ap, in_=in_.ap), reads=[in_], writes=[out])

    def memset(self, eng, out, val):
        return self.op(eng, lambda e: e.memset(out.ap, val), writes=[out])

    def emit(self, final_waits):
        nc = self.nc
        cnt = {e: 0 for e in ENGS}
        for e in ENGS:
            for op in self.ops[e]:
                if not op.is_dma:
                    op.sem = self.esem[e]
                    if op.signal:
                        cnt[e] += 1
                        op.val = cnt[e]
        ops = self.ops

        def run(eng_name, e):
            seen = {}
            for op in ops[eng_name]:
                for d in op.deps:
                    key = id(d.sem)
                    if seen.get(key, 0) < d.val:
                        e.wait_ge(d.sem, d.val)
                        seen[key] = d.val
                inst = op.fn(e)
                if op.signal:
                    inst.then_inc(op.sem, op.inc)
            if eng_name == "sp":
                for d in final_waits:
                    e.wait_ge(d.sem, d.val)

        with nc.Block() as block:
            @block.sync
            def _(e):
                run("sp", e)

            @block.tensor
            def _(e):
                run("pe", e)

            @block.scalar
            def _(e):
                run("act", e)

            @block.vector
            def _(e):
                run("dve", e)

            @block.gpsimd
            def _(e):
                run("pool", e)


class Arena:
    CELL = 256

    def __init__(self, handle, total):
        self.h = handle
        self.tks = [Tk(f"ar{i}") for i in range(total // self.CELL + 1)]

    def v(self, off, n, dt=BF16):
        nb = n * 2 if dt == F32 else n
        ap = self.h[:, off:off + nb]
        if dt == F32:
            ap = ap.bitcast(F32)
        return V(ap, self.tks[off // self.CELL:(off + nb - 1) // self.CELL + 1])


class Reg:
    def __init__(self, A, base, width, dt=BF16, stride=None):
        self.A, self.base, self.width, self.dt = A, base, width, dt
        self.es = 2 if dt == F32 else 1
        self.stride = (stride if stride is not None else width) * self.es

    def t(self, i, a=0, b=None):
        b = self.width if b is None else b
        return self.A.v(self.base + i * self.stride + a * self.es, b - a, self.dt)

    def multi(self, i0, n):
        v = self.A.v(self.base + i0 * self.stride, n * self.width, self.dt)
        return V(v.ap.rearrange("p (k c) -> p k c", c=self.width), v.tks)


class PsumPool:
    def __init__(self, handle):
        self.buf = Buf(handle, 8, "ps")
        self.i = 0

    nrot = 8

    def get(self, n=512):
        k = self.i % self.nrot
        self.i += 1
        return self.buf.s(k, slice(0, n))

    def v(self, k, a, b):
        return self.buf.s(k, slice(a, b))


class WStream:
    def __init__(self, S, A, base, nslots, slot_elems=4096):
        self.S, self.A, self.base, self.n, self.se = S, A, base, nslots, slot_elems
        self.i = 0

    def load(self, w2d, k0, nk, c0, ncols):
        assert nk * ncols <= self.se
        off = self.base + (self.i % self.n) * self.se
        self.i += 1
        slot = self.A.v(off, nk * ncols)
        out_ap = slot.ap.rearrange("p (k c) -> p k c", c=ncols)
        in_ap = w2d.rearrange("(k p) c -> p k c", p=128)[:, k0:k0 + nk, c0:c0 + ncols]
        self.S.dma("pool", out_ap, in_ap, writes=[slot])
        A = self.A

        def blk(k, j0, j1):
            return A.v(off + k * ncols + j0, j1 - j0)
        return blk


VO = {}
_c = 0
for _l in range(2):
    for _j in range(3):
        VO["lng", _l, _j] = _c; _c += 16
        VO["lnb", _l, _j] = _c; _c += 16
VO["b_pw1"] = _c; _c += 32
for _n in ("b_dw", "cln_g", "cln_b", "b_pw2"):
    VO[_n] = _c; _c += 16
VO["sub_g"] = _c; _c += 8
VO["w_dw"] = _c; _c += 31 * 16
VO["lam"] = _c; _c += 4
VO["hmask"] = _c; _c += 1
NV = _c


def fm(v):
    return np.ascontiguousarray(np.asarray(v, np.float32).reshape(-1, 128).T)


def pack_vecs(inp, hmask):
    out = np.zeros((128, NV), np.float32)
    for l in range(2):
        for j in range(3):
            out[:, VO["lng", l, j]:VO["lng", l, j] + 16] = fm(inp["ln_g"][l, j])
            out[:, VO["lnb", l, j]:VO["lnb", l, j] + 16] = fm(inp["ln_b"][l, j])
    out[:, VO["b_pw1"]:VO["b_pw1"] + 32] = fm(inp["conv_b_pw1"][0])
    out[:, VO["b_dw"]:VO["b_dw"] + 16] = fm(inp["conv_b_dw"][0])
    out[:, VO["cln_g"]:VO["cln_g"] + 16] = fm(inp["conv_ln_g"][0])
    out[:, VO["cln_b"]:VO["cln_b"] + 16] = fm(inp["conv_ln_b"][0])
    out[:, VO["b_pw2"]:VO["b_pw2"] + 16] = fm(inp["conv_b_pw2"][0])
    out[:, VO["sub_g"]:VO["sub_g"] + 8] = fm(inp["diff_subln_g"][0])
    for k in range(31):
        out[:, VO["w_dw"] + k * 16:VO["w_dw"] + (k + 1) * 16] = fm(inp["conv_w_dw"][0, k])
    out[:, VO["lam"]:VO["lam"] + 4] = np.asarray(inp["diff_lambda"][0], np.float32).T
    out[:, VO["hmask"]] = hmask
    return out


G = 512
T32 = 1024
OFF_H = 0
OFF_KV = OFF_H + 16 * T32
OFF_WB = OFF_KV + 4 * 4096
OFF_HB = OFF_WB + 3 * 8192
OFF_SQ = OFF_HB + 16 * G
OFF_ST = OFF_SQ + 4 * G
OFF_UT = OFF_ST + 4 * T32
OFF_BIG = OFF_UT + 1024
BIG_SZ = 50 * 512
AR_TOTAL = OFF_BIG + BIG_SZ


class P2:
    def __init__(self, S, A, PS, dram, vecs, consts, NG, do_l0, do_l1):
        self.S, self.A, self.PS, self.d, self.NG = S, A, PS, dram, NG
        self.do_l0, self.do_l1 = do_l0, do_l1
        self.VEC = vecs
        self.ONES, self.IDENT, self.ONESF, self.SM = consts
        self.H = Reg(A, OFF_H, G, F32)
        self.HB = Reg(A, OFF_HB, G)
        self.SQ = Reg(A, OFF_SQ, G)
        self.ST = Reg(A, OFF_ST, G, F32)
        self.BIG = Reg(A, OFF_BIG, G)
        self.KT = [Reg(A, OFF_KV + l * 8192, 256) for l in range(2)]
        self.VM = [Reg(A, OFF_KV + l * 8192 + 4096, 2048) for l in range(2)]
        self.WS = WStream(S, A, OFF_WB, 6)
        self.sqi = 0
        self.out_dmas = []

    def vcol(self, c, n=1):
        return V(self.VEC.h[:, c:c + n], self.VEC.tks)

    def layer_norm(self, gcol, bcol, src=None, dst32=None, dstb=None, silu=False):
        S, PS = self.S, self.PS
        src = src or self.H
        ps1, ps2 = PS.get(), PS.get()
        for c in range(KC):
            zb = self.SQ.t(self.sqi % 4); z2 = self.SQ.t((self.sqi + 1) % 4); self.sqi += 2
            S.act(zb, src.t(c), AF.Copy)
            S.act(z2, src.t(c), AF.Square)
            S.mm(ps1, self.ONES, zb, c == 0, c == KC - 1)
            S.mm(ps2, self.ONES, z2, c == 0, c == KC - 1)
        mean, msq, rstd = self.ST.t(0), self.ST.t(1), self.ST.t(2)
        S.ts("dve", mean, ps1, 1.0 / D, None, ALU.mult)
        S.tt("dve", msq, mean, mean, ALU.mult)
        S.stt("dve", rstd, ps2, 1.0 / D, msq, ALU.mult, ALU.subtract)
        S.ts("dve", rstd, rstd, 1.0, LN_EPS, ALU.mult, ALU.add)
        S.act(rstd, rstd, AF.Sqrt)
        S.op("dve", lambda e: e.reciprocal(out=rstd.ap, in_=rstd.ap), reads=[rstd], writes=[rstd])
        for c in range(KC):
            x = src.t(c)
            o32 = dst32.t(c) if dst32 is not None else x
            S.tt("dve", x, x, mean, ALU.subtract)
            S.tt("dve", x, x, rstd, ALU.mult)
            if silu:
                S.act(dstb.t(c), x, AF.Silu, scale=self.vcol(gcol + c), bias=self.vcol(bcol + c))
            else:
                S.act(o32, x, AF.Identity, scale=self.vcol(gcol + c), bias=self.vcol(bcol + c))
                S.act((dstb or self.HB).t(c), o32, AF.Copy)

    def linear(self, w2d, nk, col0, noc, rhs_fn, evac, n=G, cb=256, ksplit=1):
        S, PS = self.S, self.PS
        per = cb // 128
        kper = nk // ksplit
        for b0 in range(0, noc, per):
            nb = min(per, noc - b0)
            blks = [self.WS.load(w2d, ks * kper, kper, col0 + b0 * 128, nb * 128) for ks in range(ksplit)]
            for j in range(nb):
                ps = PS.get(n)
                for k in range(nk):
                    S.mm(ps, blks[k // kper](k % kper, j * 128, (j + 1) * 128), rhs_fn(k), k == 0, k == nk - 1)
                evac(b0 + j, ps)

    def resid_evac(self, oc, ps):
        h = self.H.t(oc)
        self.S.stt("dve", h, h, ALPHA, ps, ALU.mult, ALU.add)

    def mem_kv(self, l, MB):
        S, PS = self.S, self.PS
        KT, VM = self.KT[l], self.VM[l]
        self.linear(self.d["xattn_wk"][l], KC, 0, KC, lambda k: MB.t(k),
                    lambda oc, ps: S.act(KT.t(oc), ps, AF.Copy), n=256)
        wv = self.d["xattn_wv"][l]
        for cbk in range(8):
            blk = self.WS.load(wv, 0, KC, cbk * 256, 256)
            for mt in range(2):
                ps = PS.get(256)
                for k in range(KC):
                    S.mm(ps, MB.t(k, mt * 128, (mt + 1) * 128), blk(k, 0, 256), k == 0, k == KC - 1)
                S.act(VM.t(mt, cbk * 256, (cbk + 1) * 256), ps, AF.Copy)

    def xattn(self, l):
        S, PS, BIG = self.S, self.PS, self.BIG
        KT, VM = self.KT[l], self.VM[l]
        self.linear(self.d["xattn_wq"][l], KC, 0, KC, lambda k: self.HB.t(k),
                    lambda oc, ps: S.act(BIG.t(oc), ps, AF.Copy))
        for hd in range(4):
            pts = []
            for mt in range(2):
                ps = PS.get()
                for c in range(4):
                    S.mm(ps, KT.t(hd * 4 + c, mt * 128, (mt + 1) * 128), BIG.t(hd * 4 + c), c == 0, c == 3)
                pt = BIG.t(32 + (hd % 2) * 2 + mt)
                S.act(pt, ps, AF.Exp, scale=XSCALE)
                pts.append(pt)
            psl = PS.get()
            for mt in range(2):
                S.mm(psl, self.ONES, pts[mt], mt == 0, mt == 1)
            rl = self.ST.t(3)
            S.op("dve", lambda e, rl=rl, psl=psl: e.reciprocal(out=rl.ap, in_=psl.ap), reads=[psl], writes=[rl])
            for c in range(4):
                ps = PS.get()
                for mt in range(2):
                    S.mm(ps, VM.t(mt, (hd * 4 + c) * 128, (hd * 4 + c + 1) * 128), pts[mt], mt == 0, mt == 1)
                S.tt("dve", BIG.t(16 + hd * 4 + c), ps, rl, ALU.mult)
        self.linear(self.d["xattn_wo"][l], KC, 0, KC, lambda k: BIG.t(16 + k), self.resid_evac)
        self.layer_norm(VO["lng", l, 1], VO["lnb", l, 1])

    def ffn(self, l):
        S, PS, BIG = self.S, self.PS, self.BIG
        w1 = self.d["ffn_w_in"][l]
        for b in range(FC // 2):
            blk1 = self.WS.load(w1, 0, KC, b * 256, 256)
            blk2 = self.WS.load(w1, 0, KC, DFF + b * 256, 256)
            for j in range(2):
                p1, p2 = PS.get(), PS.get()
                for k in range(KC):
                    S.mm(p1, blk1(k, j * 128, (j + 1) * 128), self.HB.t(k), k == 0, k == KC - 1)
                for k in range(KC):
                    S.mm(p2, blk2(k, j * 128, (j + 1) * 128), self.HB.t(k), k == 0, k == KC - 1)
                sl = self.ST.t(2 + (j % 2))
                S.act(sl, p1, AF.Silu)
                S.tt("dve", BIG.t(b * 2 + j), sl, p2, ALU.mult)
        self.linear(self.d["ffn_w_out"][l], FC, 0, KC, lambda k: BIG.t(k), self.resid_evac, cb=128, ksplit=2)
        self.layer_norm(VO["lng", l, 2], VO["lnb", l, 2])

    def mix_tail(self, t0):
        S, PS, BIG, H = self.S, self.PS, self.BIG, self.H
        d = self.d
        fmv = lambda ap, n: ap.rearrange("(k p) t -> p k t", p=128)[:, 0:n, t0:t0 + G]
        S.dma("sp", H.multi(0, 8).ap, fmv(d["o0T"], 8), writes=[H.multi(0, 8)])
        S.dma("sp", H.multi(8, 8).ap, fmv(d["o1T"], 8), writes=[H.multi(8, 8)])
        S.dma("pool", BIG.multi(0, 8).ap, fmv(d["retT"], 8), writes=[BIG.multi(0, 8)])
        neglam = V(self.SM.h[:, 0:1], self.SM.tks)
        for hd in range(4):
            pss = PS.get()
            for j in range(2):
                c = hd * 2 + j
                S.stt("dve", H.t(c), H.t(8 + c), neglam, H.t(c), ALU.mult, ALU.add)
                z2 = self.SQ.t(self.sqi % 4); self.sqi += 1
                S.act(z2, H.t(c), AF.Square)
                S.mm(pss, self.ONES, z2, j == 0, j == 1)
            rstd = self.ST.t(hd % 2)
            S.ts("dve", rstd, pss, 1.0 / 256, LN_EPS, ALU.mult, ALU.add)
            S.act(rstd, rstd, AF.Sqrt)
            S.op("dve", lambda e, r=rstd: e.reciprocal(out=r.ap, in_=r.ap), reads=[rstd], writes=[rstd])
            for j in range(2):
                c = hd * 2 + j
                S.stt("dve", BIG.t(8 + c), H.t(c), self.vcol(VO["sub_g"] + c), rstd, ALU.mult, ALU.mult)
        S.dma("sp", H.multi(0, 16).ap, fmv(d["xT"], 16), writes=[H.multi(0, 16)])
        self.linear(d["w_mix_out"], KC, 0, KC, lambda k: BIG.t(k), self.resid_evac)
        self.layer_norm(VO["lng", 0, 0], VO["lnb", 0, 0])

    def conv_module(self, g, t0):
        S, PS, BIG, H, HB = self.S, self.PS, self.BIG, self.H, self.HB
        d = self.d
        UW = 32 + G
        UB = Reg(self.A, OFF_BIG, UW)
        DG = Reg(self.A, OFF_BIG + 8960, 128, stride=256)
        Y2 = Reg(self.A, OFF_BIG + 8960 + 31 * 256, G, F32)
        UT = Reg(self.A, OFF_UT, 32)
        HH = Reg(self.A, OFF_UT + 512, 32)
        w1 = d["conv_w_pw1"]
        bcol = VO["b_pw1"]
        hmask = self.vcol(VO["hmask"])
        for b in range(8):
            blka = self.WS.load(w1, 0, KC, b * 256, 256)
            blkg = self.WS.load(w1, 0, KC, D + b * 256, 256)
            for j in range(2):
                oc = b * 2 + j
                pa, pg = PS.get(), PS.get()
                for k in range(KC):
                    S.mm(pa, blka(k, j * 128, (j + 1) * 128), HB.t(k), k == 0, k == KC - 1)
                for k in range(KC):
                    S.mm(pg, blkg(k, j * 128, (j + 1) * 128), HB.t(k), k == 0, k == KC - 1)
                sg = self.ST.t(2 + (j % 2))
                S.act(sg, pg, AF.Sigmoid, bias=self.vcol(bcol + 16 + oc))
                S.stt("dve", UB.t(oc, 32, UW), pa, self.vcol(bcol + oc), sg, ALU.add, ALU.mult)
                if g == 0:
                    pa2, pg2 = PS.get(32), PS.get(32)
                    for k in range(KC):
                        S.mm(pa2, blka(k, j * 128, (j + 1) * 128), HH.t(k), k == 0, k == KC - 1)
                    for k in range(KC):
                        S.mm(pg2, blkg(k, j * 128, (j + 1) * 128), HH.t(k), k == 0, k == KC - 1)
                    sg2 = self.ST.t(3, 0, 32)
                    S.act(sg2, pg2, AF.Sigmoid, bias=self.vcol(bcol + 16 + oc))
                    S.stt("dve", sg2, pa2, self.vcol(bcol + oc), sg2, ALU.add, ALU.mult)
                    S.ts("dve", UB.t(oc, 0, 32), sg2, hmask, None, ALU.mult)
                else:
                    S.copy("dve", UB.t(oc, 0, 32), UT.t(oc))
        Y1 = Reg(self.A, OFF_HB, G, F32)
        ytile = lambda c: Y1.t(c) if c < 8 else Y2.t(c - 8)
        for c in range(KC):
            ps = PS.get()
            for k in range(31):
                dg = DG.t(k)
                S.ts("dve", dg, self.IDENT, self.vcol(VO["w_dw"] + k * 16 + c), None, ALU.mult)
                S.mm(ps, dg, UB.t(c, 2 + k, 2 + k + G), k == 0, k == 30)
            S.act(ytile(c), ps, AF.Identity, bias=self.vcol(VO["b_dw"] + c))
            S.copy("dve", UT.t(c), UB.t(c, G, UW))

        class _Y:
            t = staticmethod(lambda c, *a: ytile(c))
        CN = Reg(self.A, OFF_BIG, G)
        self.layer_norm(VO["cln_g"], VO["cln_b"], src=_Y, dstb=CN, silu=True)
        def evac(oc, ps):
            h = H.t(oc)
            tmp = self.ST.t(2 + (oc % 2))
            S.act(tmp, ps, AF.Identity, bias=self.vcol(VO["b_pw2"] + oc))
            S.stt("dve", h, h, ALPHA, tmp, ALU.mult, ALU.add)
        self.linear(d["conv_w_pw2"], KC, 0, KC, lambda k: CN.t(k), evac)
        self.layer_norm(VO["lng", 1, 0], VO["lnb", 1, 0])

    def prologue(self):
        S, PS = self.S, self.PS
        S.memset("dve", self.ONES, 1.0)
        S.memset("dve", self.ONESF, 1.0)
        S.dma("pool", self.IDENT.ap, self.d["ident"], writes=[self.IDENT])
        S.dma("sp", self.VEC.h[:], self.d["vecs"], writes=[self.VEC.all()])
        if self.do_l0:
            pr = V(self.SM.h[:, 2:4], self.SM.tks)
            S.tt("dve", V(self.SM.h[:, 2:3], self.SM.tks), self.vcol(VO["lam"]), self.vcol(VO["lam"] + 1), ALU.mult)
            S.tt("dve", V(self.SM.h[:, 3:4], self.SM.tks), self.vcol(VO["lam"] + 2), self.vcol(VO["lam"] + 3), ALU.mult)
            ps = PS.get(2)
            S.mm(ps, self.ONESF, pr, True, True)
            ex = V(self.SM.h[:, 4:6], self.SM.tks)
            S.act(ex, ps, AF.Exp)
            nl = V(self.SM.h[:, 0:1], self.SM.tks)
            S.tt("dve", nl, V(self.SM.h[:, 5:6], self.SM.tks), V(self.SM.h[:, 4:5], self.SM.tks), ALU.subtract)
            S.ts("dve", nl, nl, -LAMBDA_INIT, None, ALU.add)
            sg = self.vcol(VO["sub_g"], 8)
            S.ts("dve", sg, sg, 1.0 - LAMBDA_INIT, None, ALU.mult)
        MB = Reg(self.A, OFF_BIG, 256)
        mbv = MB.multi(0, 16)
        S.dma("pool", mbv.ap, self.d["memT"].rearrange("(k p) m -> p k m", p=128), writes=[mbv])
        for l in ([0] if self.do_l0 else []) + ([1] if self.do_l1 else []):
            self.mem_kv(l, MB)

    def run(self):
        S, H, d = self.S, self.H, self.d
        self.prologue()
        for g in range(self.NG):
            t0 = g * G
            fmv = lambda ap: ap.rearrange("(k p) t -> p k t", p=128)[:, :, t0:t0 + G]
            if self.do_l0:
                self.mix_tail(t0)
                self.xattn(0)
                self.ffn(0)
            if self.do_l1:
                if not self.do_l0:
                    S.dma("sp", H.multi(0, 16).ap, fmv(d["h1T"]), writes=[H.multi(0, 16)])
                    for c in range(KC):
                        S.act(self.HB.t(c), H.t(c), AF.Copy)
                if g == 0:
                    HH = Reg(self.A, OFF_UT + 512, 32)
                    hv = HH.multi(0, 16)
                    S.dma("pool", hv.ap, d["haloT"].rearrange("(k p) t -> p k t", p=128), writes=[hv])
                self.conv_module(g, t0)
                self.xattn(1)
                self.ffn(1)
            outname = "outT" if self.do_l1 else "h1T_out"
            self.out_dmas.append(S.dma("sp", fmv(d[outname]), H.multi(0, 16).ap, reads=[H.multi(0, 16)]))


def build_p2(NG, do_l0, do_l1):
    nc = bass.Bass("TRN2", target_bir_lowering=False)
    NT = NG * G
    d = {}

    def inp(name, shape):
        d[name] = nc.dram_tensor(name, shape, F32, kind="ExternalInput").ap()

    inp("vecs", [128, NV]); inp("ident", [128, 128]); inp("memT", [D, NMEM])
    for n in ("xattn_wq", "xattn_wk", "xattn_wv", "xattn_wo"):
        inp(n, [2, D, D])
    inp("ffn_w_in", [2, D, 2 * DFF]); inp("ffn_w_out", [2, DFF, D])
    if do_l0:
        inp("xT", [D, NT]); inp("retT", [1024, NT]); inp("o0T", [1024, NT]); inp("o1T", [1024, NT])
        inp("w_mix_out", [D, D])
    if do_l1:
        inp("conv_w_pw1", [D, 2 * D]); inp("conv_w_pw2", [D, D]); inp("haloT", [D, 32])
        if not do_l0:
            inp("h1T", [D, NT])
    outname = "outT" if do_l1 else "h1T_out"
    d[outname] = nc.dram_tensor(outname, [D, NT], F32, kind="ExternalOutput").ap()
    with ExitStack() as es:
        ar = es.enter_context(nc.sbuf_tensor("ar", [128, AR_TOTAL], BF16))
        vec = es.enter_context(nc.sbuf_tensor("vec", [128, NV], F32))
        ones = es.enter_context(nc.sbuf_tensor("ones", [128, 128], BF16))
        ident = es.enter_context(nc.sbuf_tensor("identb", [128, 128], BF16))
        onesf = es.enter_context(nc.sbuf_tensor("onesf", [128, 128], F32))
        sm = es.enter_context(nc.sbuf_tensor("sm", [128, 8], F32))
        ps = es.enter_context(nc.psum_tensor("ps", [128, 8, 512], F32))
        S = Sched(nc, es)
        A = Arena(ar, AR_TOTAL)
        consts = (V(ones[:], [Tk("ones")]), V(ident[:], [Tk("ident")]), V(onesf[:], [Tk("onesf")]), Buf(sm, 1, "sm"))
        p2 = P2(S, A, PsumPool(ps), d, Buf(vec, 1, "vec"), consts, NG, do_l0, do_l1)
        p2.run()
        S.emit(p2.out_dmas)
    return nc


TB_OFF, TB_DIAG, TB_CT, TB_DM, TB_XI, TB_SC = 0, 512, 2560, 2688, 2816, 2944
NTAB = 2948


def p1_tables(core, inp):
    h = core
    gamma = 1.0 - 2.0 ** (-5.0 - h)
    lg = np.log1p(-(2.0 ** (-5.0 - h)))
    slope = 2.0 ** (-8.0 * (core // 2 + 1) / 4)
    tab = np.zeros((128, NTAB), np.float64)
    p = np.arange(128)[:, None]
    f = np.arange(512)[None, :]
    tab[:, TB_OFF:TB_OFF + 512] = -slope * (f - p)
    for r in range(4):
        dist = f - p - 128 * r
        tab[:, TB_DIAG + r * 512:TB_DIAG + (r + 1) * 512] = np.where(dist >= 0, -slope * dist, -30000.0)
    tab[:, TB_CT:TB_CT + 128] = -slope * 128.0 * np.arange(128)[None, :]
    i = np.arange(128)[None, :]
    tab[:, TB_DM:TB_DM + 128] = np.where(i - p >= 0, np.exp(lg * np.maximum(i - p, 0)), 0.0) * (128 ** -0.5)
    tab[:, TB_XI:TB_XI + 128] = np.exp(lg * (i + 1.0))
    tab[:, TB_SC + 0] = np.exp(lg * (127.0 - np.arange(128))) * (128 ** -0.5)
    tab[:, TB_SC + 1] = np.exp(lg * 128.0)
    tab[:, TB_SC + 2] = np.asarray(inp["ret_gn_g"][0, h * 128:(h + 1) * 128], np.float64)
    tab[:, TB_SC + 3] = np.asarray(inp["ret_gn_b"][0, h * 128:(h + 1) * 128], np.float64)
    return tab.astype(np.float32)


def p1_weights(core, w_in):
    h, hd, m = core, core // 2, core % 2
    sl = lambda a: w_in[:, a:a + 128]
    wf = np.concatenate([sl(h * 128), sl(1024 + h * 128), sl(3072 + h * 128),
                         sl(4096 + hd * 256 + m * 128), sl(5120 + hd * 256 + m * 128)], 1)
    wt = np.concatenate([sl(1024 + h * 128), sl(2048 + h * 128), w_in[:, 6144 + hd * 256:6144 + (hd + 1) * 256]], 1)
    return np.ascontiguousarray(wf), np.ascontiguousarray(wt)


class P1:
    parts = ('tm', 'ev1', 'ev2', 'ev3', 'ret', 'gn', 'gn2', 'att', 'att2')

    def __init__(self, S, A, PS, d, tab, ones, NTOK):
        self.S, self.A, self.PS, self.d, self.TAB, self.ONES, self.NTOK = S, A, PS, d, tab, ones, NTOK
        self.out_dmas = []

    def tb(self, c, n=1):
        return V(self.TAB.h[:, c:c + n], self.TAB.tks)

    def run(self):
        S, A, PS, d, NTOK = self.S, self.A, self.PS, self.d, self.NTOK
        o = 0
        def reg(n, width, dt=BF16, stride=None):
            nonlocal o
            o = -(-o // 256) * 256
            r = Reg(A, o, width, dt, stride)
            o += n * (stride or width) * (2 if dt == F32 else 1)
            return r
        DKT = reg(1, NTOK)
        DV = reg(NTOK // 128, 256)
        WF = reg(16, 640)
        WT = reg(16, 512)
        XB = reg(16, 512)
        RQ, RKF, DQ = reg(1, 512), reg(1, 512), reg(1, 512)
        RG, YR = reg(1, 512, F32), reg(1, 512, F32)
        RKZ, TM = reg(4, 128, stride=256), reg(4, 512)
        SMT, QX = reg(2, 128, stride=256), reg(2, 128, stride=256)
        STATE = reg(1, 128, F32)
        STATEB = reg(1, 128)
        NTT = 5
        TT = reg(NTT, 512, F32)
        PP = reg(NTT, 512)
        YB = reg(2, 512)
        ST = reg(3, 512, F32)
        OO = reg(2, 512, F32)
        assert o <= 3 * NTOK + 50176, o
        PS.nrot = 5
        po0, po1, pl = PS.v(5, 0, 512), PS.v(6, 0, 512), PS.v(7, 0, 512)

        S.memset("dve", self.ONES, 1.0)
        S.dma("sp", self.TAB.h[:], d["tab"], writes=[self.TAB.all()])
        S.dma("pool", WF.multi(0, 16).ap, d["wf"].rearrange("(k p) c -> p k c", p=128), writes=[WF.multi(0, 16)])
        S.dma("pool", WT.multi(0, 16).ap, d["wt"].rearrange("(k p) c -> p k c", p=128), writes=[WT.multi(0, 16)])
        S.memset("dve", STATE.t(0), 0.0)
        S.memset("dve", STATEB.t(0), 0.0)
        QS = 128 ** -0.5
        for g in range(NTOK // 512):
            t0 = g * 512
            xv = XB.multi(0, 16)
            S.dma("pool", xv.ap, d["xT"].rearrange("(k p) t -> p k t", p=128)[:, :, t0:t0 + 512], writes=[xv])
            for j in range(5):
                ps = PS.get()
                for k in range(16):
                    S.mm(ps, WF.t(k, j * 128, (j + 1) * 128), XB.t(k), k == 0, k == 15)
                if j == 0:
                    S.act(RQ.t(0), ps, AF.Copy)
                elif j == 1:
                    S.act(RKF.t(0), ps, AF.Copy)
                elif j == 2:
                    S.act(RG.t(0), ps, AF.Silu)
                elif j == 3:
                    S.act(DQ.t(0), ps, AF.Copy, scale=QS)
                else:
                    S.act(DKT.t(0, t0, t0 + 512), ps, AF.Copy)
            for tt in (range(4) if 'tm' in self.parts else []):
                ps = PS.get()
                k_ = PS.i - 1
                for k in range(16):
                    S.mm(ps, XB.t(k, tt * 128, (tt + 1) * 128), WT.t(k), k == 0, k == 15)
                kk = k_ % PS.nrot
                S.act(TM.t(tt), ps, AF.Copy)
                S.ts("dve", RKZ.t(tt), TM.t(tt, 0, 128), self.tb(TB_SC + 0), None, ALU.mult)
                S.copy("dve", DV.t(4 * g + tt), TM.t(tt, 256, 512))
            for tt in (range(4) if 'ret' in self.parts else []):
                a, b = tt * 128, (tt + 1) * 128
                pss = PS.get(128)
                S.mm(pss, RKF.t(0, a, b), RQ.t(0, a, b), True, True)
                sm = SMT.t(tt % 2); qx = QX.t(tt % 2)
                S.tt("dve", sm, pss, self.tb(TB_DM, 128), ALU.mult)
                S.tt("dve", qx, RQ.t(0, a, b), self.tb(TB_XI, 128), ALU.mult)
                psy = PS.get(128)
                S.mm(psy, TM.t(tt, 128, 256), sm, True, False)
                S.mm(psy, STATEB.t(0), qx, False, True)
                S.act(YR.t(0, a, b), psy, AF.Copy)
                pkv = PS.get(128)
                S.mm(pkv, RKZ.t(tt), TM.t(tt, 128, 256), True, True)
                S.stt("dve", STATE.t(0), STATE.t(0), self.tb(TB_SC + 1), pkv, ALU.mult, ALU.add)
                S.act(STATEB.t(0), STATE.t(0), AF.Copy)
            y = YR.t(0)
            if 'ret' not in self.parts:
                S.memset('dve', y, 0.0)
            S.act(YB.t(0), y, AF.Copy)
            S.act(YB.t(1), y, AF.Square)
            ps1, ps2 = PS.get(), PS.get()
            S.mm(ps1, self.ONES, YB.t(0), True, True)
            S.mm(ps2, self.ONES, YB.t(1), True, True)
            mean, msq, rstd = ST.t(0), ST.t(1), ST.t(2)
            S.ts("dve", mean, ps1, 1.0 / 128, None, ALU.mult)
            if 'gn' not in self.parts:
                S.memset('dve', rstd, 1.0)
            S.tt("dve", msq, mean, mean, ALU.mult)
            S.stt("dve", rstd, ps2, 1.0 / 128, msq, ALU.mult, ALU.subtract)
            S.ts("dve", rstd, rstd, 1.0, LN_EPS, ALU.mult, ALU.add)
            S.act(rstd, rstd, AF.Sqrt)
            S.op("dve", lambda e, r=rstd: e.reciprocal(out=r.ap, in_=r.ap), reads=[rstd], writes=[rstd])
            S.tt("dve", y, y, mean, ALU.subtract)
            S.tt("dve", y, y, rstd, ALU.mult)
            S.act(y, y, AF.Identity, scale=self.tb(TB_SC + 2), bias=self.tb(TB_SC + 3))
            S.tt("dve", y, y, RG.t(0), ALU.mult)
            self.out_dmas.append(S.dma("sp", d["ret_out"][:, t0:t0 + 512], y.ap, reads=[y]))
            nj = 4 * g + 4
            if 'att2' not in self.parts:
                nj = 1
            if 'att' not in self.parts:
                nj = 0
            AHEAD = 3
            pend = {}

            def score(j):
                ps = PS.get()
                S.mm(ps, DKT.t(0, j * 128, (j + 1) * 128), DQ.t(0), True, True)
                r = j - 4 * g
                bt = self.tb(TB_OFF, 512) if r < 0 else self.tb(TB_DIAG + r * 512, 512)
                t = TT.t(j % NTT); p = PP.t(j % NTT)
                S.tt("dve", t, ps, bt, ALU.add)
                if r < 0:
                    S.act(p, t, AF.Exp, bias=self.tb(TB_CT + (4 * g - j)))
                else:
                    S.act(p, t, AF.Exp)
                pend[j] = p

            for j in range(min(AHEAD, nj)):
                score(j)
            for j in range(nj):
                if j + AHEAD < nj:
                    score(j + AHEAD)
                p = pend.pop(j)
                S.mm(po0, DV.t(j, 0, 128), p, j == 0, j == nj - 1)
                S.mm(po1, DV.t(j, 128, 256), p, j == 0, j == nj - 1)
                S.mm(pl, self.ONES, p, j == 0, j == nj - 1)
            rl = ST.t(0)
            S.op("dve", lambda e, rl=rl: e.reciprocal(out=rl.ap, in_=pl.ap), reads=[pl], writes=[rl])
            S.tt("dve", OO.t(0), po0, rl, ALU.mult)
            S.tt("dve", OO.t(1), po1, rl, ALU.mult)
            ov = OO.multi(0, 2)
            self.out_dmas.append(S.dma("sp", d["o_out"].rearrange("(k p) t -> p k t", p=128)[:, :, t0:t0 + 512], ov.ap, reads=[ov]))


def build_p1(NTOK):
    nc = bass.Bass("TRN2", target_bir_lowering=False)
    d = {}
    for name, shape in (("xT", [D, NTOK]), ("wf", [D, 640]), ("wt", [D, 512]), ("tab", [128, NTAB])):
        d[name] = nc.dram_tensor(name, shape, F32, kind="ExternalInput").ap()
    p1o = nc.dram_tensor("p1_out", [384, NTOK], F32, kind="ExternalOutput").ap()
    d["ret_out"] = p1o[0:128, :]
    d["o_out"] = p1o[128:384, :]
    ar_total = 3 * NTOK + 50176
    with ExitStack() as es:
        ar = es.enter_context(nc.sbuf_tensor("ar", [128, ar_total], BF16))
        tab = es.enter_context(nc.sbuf_tensor("tabs", [128, NTAB], F32))
        ones = es.enter_context(nc.sbuf_tensor("ones", [128, 128], BF16))
        ps = es.enter_context(nc.psum_tensor("ps", [128, 8, 512], F32))
        S = Sched(nc, es)
        A = Arena(ar, ar_total)
        p1 = P1(S, A, PsumPool(ps), d, Buf(tab, 1, "tab"), V(ones[:], [Tk("ones")]), NTOK)
        p1.run()
        S.emit(p1.out_dmas)
    return nc


_NC_CACHE = {}


def _get(key, fn):
    if key not in _NC_CACHE:
        _NC_CACHE[key] = fn()
    return _NC_CACHE[key]


def kernel(**inp):
    inp = {k: np.asarray(v) for k, v in inp.items()}
    cores = list(range(NCORES))
    TPC = SEQ // NCORES
    NG = TPC // G
    xT = np.ascontiguousarray(inp["x"][0].T)
    memT = np.ascontiguousarray(inp["mem"][0].T)
    ident = np.eye(128, dtype=np.float32)

    nc1 = _get("p1", lambda: build_p1(SEQ))
    maps = []
    for c in cores:
        wf, wt = p1_weights(c, inp["w_in"][0])
        maps.append({"xT": xT, "wf": wf, "wt": wt, "tab": p1_tables(c, inp)})
    r1 = run_bass_kernel_spmd(nc1, maps, core_ids=cores).results
    retT = np.concatenate([r1[c]["p1_out"][0:128] for c in cores], 0)
    o0T = np.concatenate([r1[2 * h]["p1_out"][128:384] for h in range(4)], 0)
    o1T = np.concatenate([r1[2 * h + 1]["p1_out"][128:384] for h in range(4)], 0)
    del r1

    common = {"ident": ident, "memT": memT}
    for n in ("xattn_wq", "xattn_wk", "xattn_wv", "xattn_wo", "ffn_w_in", "ffn_w_out"):
        common[n] = inp[n]

    nc2 = _get("p2a", lambda: build_p2(NG, True, False))
    maps = []
    for c in cores:
        sl = slice(c * TPC, (c + 1) * TPC)
        m = dict(common)
        m.update(vecs=pack_vecs(inp, 1.0), xT=np.ascontiguousarray(xT[:, sl]), retT=np.ascontiguousarray(retT[:, sl]),
                 o0T=np.ascontiguousarray(o0T[:, sl]), o1T=np.ascontiguousarray(o1T[:, sl]), w_mix_out=inp["w_mix_out"][0])
        maps.append(m)
    r2 = run_bass_kernel_spmd(nc2, maps, core_ids=cores).results
    h1 = [r2[c]["h1T_out"] for c in cores]
    del r2

    nc3 = _get("p2b", lambda: build_p2(NG, False, True))
    maps = []
    for c in cores:
        m = dict(common)
        halo = h1[c - 1][:, TPC - HALO:] if c > 0 else np.zeros((D, HALO), np.float32)
        m.update(vecs=pack_vecs(inp, 1.0 if c > 0 else 0.0), h1T=h1[c], haloT=np.ascontiguousarray(halo),
                 conv_w_pw1=inp["conv_w_pw1"][0], conv_w_pw2=inp["conv_w_pw2"][0])
        maps.append(m)
    r3 = run_bass_kernel_spmd(nc3, maps, core_ids=cores).results
    outT = np.concatenate([r3[c]["outT"] for c in cores], 1)
    return np.ascontiguousarray(outT.T)[None].astype(np.float32)
```

```python
import math
from contextlib import ExitStack

import numpy as np
import concourse.bass as bass
import concourse.mybir as mybir
from concourse.bass_utils import run_bass_kernel_spmd

F32 = mybir.dt.float32
BF16 = mybir.dt.bfloat16
AF = mybir.ActivationFunctionType
ALU = mybir.AluOpType

D = 2048
KC = 16
SEQ = 16384
NCORES = 8
NMEM = 256
DFF = 5632
FC = DFF // 128
ALPHA = (2.0 * 2) ** 0.25
LN_EPS = 1e-5
LAMBDA_INIT = 0.8 - 0.6 * math.exp(-0.3 * 1)
XSCALE = 512 ** -0.5
HALO = 32


class Tk:
    __slots__ = ("w", "r", "name")

    def __init__(self, name=""):
        self.w = None
        self.r = []
        self.name = name


class V:
    __slots__ = ("ap", "tks")

    def __init__(self, ap, tks):
        self.ap = ap
        self.tks = tks


class Buf:
    def __init__(self, handle, nslots, name):
        self.h = handle
        self.n = nslots
        self.tks = [Tk(f"{name}[{i}]") for i in range(nslots)]

    def s(self, i, *idx):
        if idx:
            return V(self.h[(slice(None), i) + idx], [self.tks[i]])
        return V(self.h[:, i], [self.tks[i]])

    def rng(self, i0, i1, *idx):
        return V(self.h[(slice(None), slice(i0, i1)) + idx], self.tks[i0:i1])

    def all(self):
        return V(self.h[:], self.tks)


class Op:
    __slots__ = ("eng", "fn", "deps", "signal", "val", "is_dma", "sem", "inc")

    def __init__(self, eng, fn, is_dma):
        self.eng = eng
        self.fn = fn
        self.deps = []
        self.signal = False
        self.val = None
        self.is_dma = is_dma
        self.sem = None
        self.inc = 1


ENGS = ("pe", "act", "dve", "pool", "sp")
NDMA = {"sp": 12, "pool": 12, "act": 4}


class Sched:
    def __init__(self, nc, es):
        self.nc = nc
        self.ops = {e: [] for e in ENGS}
        self.esem = {e: es.enter_context(nc.semaphore(f"e_{e}")) for e in ENGS}
        self.dsem = {q: [es.enter_context(nc.semaphore(f"d_{q}{i}")) for i in range(n)]
                     for q, n in NDMA.items()}
        self.dcnt = {q: 0 for q in NDMA}
        self.dlast = {q: [None] * n for q, n in NDMA.items()}

    def _add(self, op, reads, writes):
        deps = {}

        def need(d, raw):
            if d is None or d is op:
                return
            if d.is_dma or op.is_dma:
                deps[id(d)] = d
            elif d.eng == op.eng:
                if raw and op.eng != "pe":
                    deps[id(d)] = d
            else:
                deps[id(d)] = d

        for v in reads:
            for tk in v.tks:
                need(tk.w, True)
        for v in writes:
            for tk in v.tks:
                need(tk.w, False)
                for r in tk.r:
                    need(r, False)
        for v in reads:
            for tk in v.tks:
                tk.r.append(op)
        for v in writes:
            for tk in v.tks:
                tk.w = op
                tk.r = []
        for d in deps.values():
            d.signal = True
            op.deps.append(d)
        self.ops[op.eng].append(op)
        return op

    def op(self, eng, fn, reads=(), writes=()):
        return self._add(Op(eng, fn, False), reads, writes)

    def dma(self, q, out_ap, in_ap, reads=(), writes=()):
        op = Op(q, lambda e: e.dma_start(out=out_ap, in_=in_ap), True)
        n = self.dcnt[q]
        self.dcnt[q] += 1
        k = n % NDMA[q]
        op.sem = self.dsem[q][k]
        op.val = 16 * (n // NDMA[q] + 1)
        op.inc = 16
        op.signal = True
        prev = self.dlast[q][k]
        self._add(op, reads, writes)
        if prev is not None:
            op.deps.append(prev)
        self.dlast[q][k] = op
        return op

    def mm(self, out, lhsT, rhs, start, stop):
        return self.op("pe", lambda e: e.matmul(out.ap, lhsT=lhsT.ap, rhs=rhs.ap, start=start, stop=stop),
                       reads=[lhsT, rhs], writes=[out])

    def act(self, out, in_, func, bias=None, scale=None, extra_reads=(), eng="act"):
        kw = {}
        if bias is not None:
            kw["bias"] = bias.ap if isinstance(bias, V) else bias
        if scale is not None:
            kw["scale"] = scale.ap if isinstance(scale, V) else scale
        rd = [in_] + [x for x in (bias, scale) if isinstance(x, V)] + list(extra_reads)
        return self.op(eng, lambda e: e.activation(out=out.ap, in_=in_.ap, func=func, **kw),
                       reads=rd, writes=[out])

    def tt(self, eng, out, in0, in1, op):
        return self.op(eng, lambda e: e.tensor_tensor(out=out.ap, in0=in0.ap, in1=in1.ap, op=op),
                       reads=[in0, in1], writes=[out])

    def ts(self, eng, out, in0, s1, s2, op0, op1=None):
        rd = [in0] + [x for x in (s1, s2) if isinstance(x, V)]
        a1 = s1.ap if isinstance(s1, V) else s1
        a2 = s2.ap if isinstance(s2, V) else s2
        if op1 is None:
            return self.op(eng, lambda e: e.tensor_scalar(out=out.ap, in0=in0.ap, scalar1=a1, scalar2=None, op0=op0),
                           reads=rd, writes=[out])
        return self.op(eng, lambda e: e.tensor_scalar(out=out.ap, in0=in0.ap, scalar1=a1, scalar2=a2, op0=op0, op1=op1),
                       reads=rd, writes=[out])

    def stt(self, eng, out, in0, scalar, in1, op0, op1):
        rd = [in0, in1] + ([scalar] if isinstance(scalar, V) else [])
        a = scalar.ap if isinstance(scalar, V) else scalar
        return self.op(eng, lambda e: e.scalar_tensor_tensor(out=out.ap, in0=in0.ap, scalar=a, in1=in1.ap, op0=op0, op1=op1),
                       reads=rd, writes=[out])

    def copy(self, eng, out, in_):
        return self.op(eng, lambda e: e.tensor_copy(out=out.ap, in_=in_.ap), reads=[in_], writes=[out])

    def memset(self, eng, out, val):
        return self.op(eng, lambda e: e.memset(out.ap, val), writes=[out])

    def emit(self, final_waits):
        nc = self.nc
        cnt = {e: 0 for e in ENGS}
        for e in ENGS:
            for op in self.ops[e]:
                if not op.is_dma:
                    op.sem = self.esem[e]
                    if op.signal:
                        cnt[e] += 1
                        op.val = cnt[e]
        ops = self.ops

        def run(eng_name, e):
            seen = {}
            for op in ops[eng_name]:
                for d in op.deps:
                    key = id(d.sem)
                    if seen.get(key, 0) < d.val:
                        e.wait_ge(d.sem, d.val)
                        seen[key] = d.val
                inst = op.fn(e)
                if op.signal:
                    inst.then_inc(op.sem, op.inc)
            if eng_name == "sp":
                for d in final_waits:
                    e.wait_ge(d.sem, d.val)

        with nc.Block() as block:
            @block.sync
            def _(e):
                run("sp", e)

            @block.tensor
            def _(e):
                run("pe", e)

            @block.scalar
            def _(e):
                run("act", e)

            @block.vector
            def _(e):
                run("dve", e)

            @block.gpsimd
            def _(e):
                run("pool", e)


class Arena:
    CELL = 256

    def __init__(self, handle, total):
        self.h = handle
        self.tks = [Tk(f"ar{i}") for i in range(total // self.CELL + 1)]

    def v(self, off, n, dt=BF16):
        nb = n * 2 if dt == F32 else n
        ap = self.h[:, off:off + nb]
        if dt == F32:
            ap = ap.bitcast(F32)
        return V(ap, self.tks[off // self.CELL:(off + nb - 1) // self.CELL + 1])


class Reg:
    def __init__(self, A, base, width, dt=BF16, stride=None):
        self.A, self.base, self.width, self.dt = A, base, width, dt
        self.es = 2 if dt == F32 else 1
        self.stride = (stride if stride is not None else width) * self.es

    def t(self, i, a=0, b=None):
        b = self.width if b is None else b
        return self.A.v(self.base + i * self.stride + a * self.es, b - a, self.dt)

    def multi(self, i0, n):
        v = self.A.v(self.base + i0 * self.stride, n * self.width, self.dt)
        return V(v.ap.rearrange("p (k c) -> p k c", c=self.width), v.tks)


class PsumPool:
    def __init__(self, handle):
        self.buf = Buf(handle, 8, "ps")
        self.i = 0

    nrot = 8

    def get(self, n=512):
        k = self.i % self.nrot
        self.i += 1
        return self.buf.s(k, slice(0, n))

    def v(self, k, a, b):
        return self.buf.s(k, slice(a, b))


class WStream:
    def __init__(self, S, A, base, nslots, slot_elems=4096):
        self.S, self.A, self.base, self.n, self.se = S, A, base, nslots, slot_elems
        self.i = 0

    def load(self, w2d, k0, nk, c0, ncols):
        assert nk * ncols <= self.se
        off = self.base + (self.i % self.n) * self.se
        self.i += 1
        slot = self.A.v(off, nk * ncols)
        out_ap = slot.ap.rearrange("p (k c) -> p k c", c=ncols)
        in_ap = w2d.rearrange("(k p) c -> p k c", p=128)[:, k0:k0 + nk, c0:c0 + ncols]
        self.S.dma("pool", out_ap, in_ap, writes=[slot])
        A = self.A

        def blk(k, j0, j1):
            return A.v(off + k * ncols + j0, j1 - j0)
        return blk


VO = {}
_c = 0
for _l in range(2):
    for _j in range(3):
        VO["lng", _l, _j] = _c; _c += 16
        VO["lnb", _l, _j] = _c; _c += 16
VO["b_pw1"] = _c; _c += 32
for _n in ("b_dw", "cln_g", "cln_b", "b_pw2"):
    VO[_n] = _c; _c += 16
VO["sub_g"] = _c; _c += 8
VO["w_dw"] = _c; _c += 31 * 16
VO["lam"] = _c; _c += 4
VO["hmask"] = _c; _c += 1
NV = _c


def fm(v):
    return np.ascontiguousarray(np.asarray(v, np.float32).reshape(-1, 128).T)


def pack_vecs(inp, hmask):
    out = np.zeros((128, NV), np.float32)
    for l in range(2):
        for j in range(3):
            out[:, VO["lng", l, j]:VO["lng", l, j] + 16] = fm(inp["ln_g"][l, j])
            out[:, VO["lnb", l, j]:VO["lnb", l, j] + 16] = fm(inp["ln_b"][l, j])
    out[:, VO["b_pw1"]:VO["b_pw1"] + 32] = fm(inp["conv_b_pw1"][0])
    out[:, VO["b_dw"]:VO["b_dw"] + 16] = fm(inp["conv_b_dw"][0])
    out[:, VO["cln_g"]:VO["cln_g"] + 16] = fm(inp["conv_ln_g"][0])
    out[:, VO["cln_b"]:VO["cln_b"] + 16] = fm(inp["conv_ln_b"][0])
    out[:, VO["b_pw2"]:VO["b_pw2"] + 16] = fm(inp["conv_b_pw2"][0])
    out[:, VO["sub_g"]:VO["sub_g"] + 8] = fm(inp["diff_subln_g"][0])
    for k in range(31):
        out[:, VO["w_dw"] + k * 16:VO["w_dw"] + (k + 1) * 16] = fm(inp["conv_w_dw"][0, k])
    out[:, VO["lam"]:VO["lam"] + 4] = np.asarray(inp["diff_lambda"][0], np.float32).T
    out[:, VO["hmask"]] = hmask
    return out


G = 512
T32 = 1024
OFF_H = 0
OFF_KV = OFF_H + 16 * T32
OFF_WB = OFF_KV + 4 * 4096
OFF_HB = OFF_WB + 3 * 8192
OFF_SQ = OFF_HB + 16 * G
OFF_ST = OFF_SQ + 4 * G
OFF_UT = OFF_ST + 4 * T32
OFF_BIG = OFF_UT + 1024
BIG_SZ = 50 * 512
AR_TOTAL = OFF_BIG + BIG_SZ


class P2:
    def __init__(self, S, A, PS, dram, vecs, consts, NG, do_l0, do_l1):
        self.S, self.A, self.PS, self.d, self.NG = S, A, PS, dram, NG
        self.do_l0, self.do_l1 = do_l0, do_l1
        self.VEC = vecs
        self.ONES, self.IDENT, self.ONESF, self.SM = consts
        self.H = Reg(A, OFF_H, G, F32)
        self.HB = Reg(A, OFF_HB, G)
        self.SQ = Reg(A, OFF_SQ, G)
        self.ST = Reg(A, OFF_ST, G, F32)
        self.BIG = Reg(A, OFF_BIG, G)
        self.KT = [Reg(A, OFF_KV + l * 8192, 256) for l in range(2)]
        self.VM = [Reg(A, OFF_KV + l * 8192 + 4096, 2048) for l in range(2)]
        self.WS = WStream(S, A, OFF_WB, 6)
        self.sqi = 0
        self.out_dmas = []

    def vcol(self, c, n=1):
        return V(self.VEC.h[:, c:c + n], self.VEC.tks)

    def layer_norm(self, gcol, bcol, src=None, dst32=None, dstb=None, silu=False):
        S, PS = self.S, self.PS
        src = src or self.H
        ps1, ps2 = PS.get(), PS.get()
        for c in range(KC):
            zb = self.SQ.t(self.sqi % 4); z2 = self.SQ.t((self.sqi + 1) % 4); self.sqi += 2
            S.copy("dve", zb, src.t(c))
            S.act(z2, src.t(c), AF.Square)
            S.mm(ps1, self.ONES, zb, c == 0, c == KC - 1)
            S.mm(ps2, self.ONES, z2, c == 0, c == KC - 1)
        mean, msq, rstd = self.ST.t(0), self.ST.t(1), self.ST.t(2)
        S.ts("dve", mean, ps1, 1.0 / D, None, ALU.mult)
        S.tt("dve", msq, mean, mean, ALU.mult)
        S.stt("dve", rstd, ps2, 1.0 / D, msq, ALU.mult, ALU.subtract)
        S.ts("dve", rstd, rstd, 1.0, LN_EPS, ALU.mult, ALU.add)
        S.act(rstd, rstd, AF.Sqrt)
        S.op("dve", lambda e: e.reciprocal(out=rstd.ap, in_=rstd.ap), reads=[rstd], writes=[rstd])
        for c in range(KC):
            x = src.t(c)
            o32 = dst32.t(c) if dst32 is not None else x
            S.tt("dve", x, x, mean, ALU.subtract)
            S.tt("dve", x, x, rstd, ALU.mult)
            if silu:
                S.act(dstb.t(c), x, AF.Silu, scale=self.vcol(gcol + c), bias=self.vcol(bcol + c))
            else:
                S.act(o32, x, AF.Identity, scale=self.vcol(gcol + c), bias=self.vcol(bcol + c))
                S.act((dstb or self.HB).t(c), o32, AF.Copy)

    def linear(self, w2d, nk, col0, noc, rhs_fn, evac, n=G, cb=256, ksplit=1):
        S, PS = self.S, self.PS
        per = cb // 128
        kper = nk // ksplit
        for b0 in range(0, noc, per):
            nb = min(per, noc - b0)
            blks = [self.WS.load(w2d, ks * kper, kper, col0 + b0 * 128, nb * 128) for ks in range(ksplit)]
            for j in range(nb):
                ps = PS.get(n)
                for k in range(nk):
                    S.mm(ps, blks[k // kper](k % kper, j * 128, (j + 1) * 128), rhs_fn(k), k == 0, k == nk - 1)
                evac(b0 + j, ps)

    def resid_evac(self, oc, ps):
        h = self.H.t(oc)
        self.S.stt("dve", h, h, ALPHA, ps, ALU.mult, ALU.add)

    def mem_kv(self, l, MB):
        S, PS = self.S, self.PS
        KT, VM = self.KT[l], self.VM[l]
        self.linear(self.d["xattn_wk"][l], KC, 0, KC, lambda k: MB.t(k),
                    lambda oc, ps: S.act(KT.t(oc), ps, AF.Copy), n=256)
        wv = self.d["xattn_wv"][l]
        for cbk in range(8):
            blk = self.WS.load(wv, 0, KC, cbk * 256, 256)
            for mt in range(2):
                ps = PS.get(256)
                for k in range(KC):
                    S.mm(ps, MB.t(k, mt * 128, (mt + 1) * 128), blk(k, 0, 256), k == 0, k == KC - 1)
                S.act(VM.t(mt, cbk * 256, (cbk + 1) * 256), ps, AF.Copy)

    def xattn(self, l):
        S, PS, BIG = self.S, self.PS, self.BIG
        KT, VM = self.KT[l], self.VM[l]
        self.linear(self.d["xattn_wq"][l], KC, 0, KC, lambda k: self.HB.t(k),
                    lambda oc, ps: S.act(BIG.t(oc), ps, AF.Copy))
        for hd in range(4):
            pts = []
            for mt in range(2):
                ps = PS.get()
                for c in range(4):
                    S.mm(ps, KT.t(hd * 4 + c, mt * 128, (mt + 1) * 128), BIG.t(hd * 4 + c), c == 0, c == 3)
                pt = BIG.t(32 + (hd % 2) * 2 + mt)
                S.act(pt, ps, AF.Exp, scale=XSCALE)
                pts.append(pt)
            psl = PS.get()
            for mt in range(2):
                S.mm(psl, self.ONES, pts[mt], mt == 0, mt == 1)
            rl = self.ST.t(3)
            S.op("dve", lambda e, rl=rl, psl=psl: e.reciprocal(out=rl.ap, in_=psl.ap), reads=[psl], writes=[rl])
            for c in range(4):
                ps = PS.get()
                for mt in range(2):
                    S.mm(ps, VM.t(mt, (hd * 4 + c) * 128, (hd * 4 + c + 1) * 128), pts[mt], mt == 0, mt == 1)
                S.tt("dve", BIG.t(16 + hd * 4 + c), ps, rl, ALU.mult)
        self.linear(self.d["xattn_wo"][l], KC, 0, KC, lambda k: BIG.t(16 + k), self.resid_evac)
        self.layer_norm(VO["lng", l, 1], VO["lnb", l, 1])

    def ffn(self, l):
        S, PS, BIG = self.S, self.PS, self.BIG
        w1 = self.d["ffn_w_in"][l]
        for b in range(FC // 2):
            blk1 = self.WS.load(w1, 0, KC, b * 256, 256)
            blk2 = self.WS.load(w1, 0, KC, DFF + b * 256, 256)
            for j in range(2):
                p1, p2 = PS.get(), PS.get()
                for k in range(KC):
                    S.mm(p1, blk1(k, j * 128, (j + 1) * 128), self.HB.t(k), k == 0, k == KC - 1)
                for k in range(KC):
                    S.mm(p2, blk2(k, j * 128, (j + 1) * 128), self.HB.t(k), k == 0, k == KC - 1)
                sl = self.ST.t(2 + (j % 2))
                S.act(sl, p1, AF.Silu)
                S.tt("dve", BIG.t(b * 2 + j), sl, p2, ALU.mult)
        self.linear(self.d["ffn_w_out"][l], FC, 0, KC, lambda k: BIG.t(k), self.resid_evac, cb=128, ksplit=2)
        self.layer_norm(VO["lng", l, 2], VO["lnb", l, 2])

    def mix_tail(self, t0):
        S, PS, BIG, H = self.S, self.PS, self.BIG, self.H
        d = self.d
        fmv = lambda ap, n: ap.rearrange("(k p) t -> p k t", p=128)[:, 0:n, t0:t0 + G]
        S.dma("sp", H.multi(0, 8).ap, fmv(d["o0T"], 8), writes=[H.multi(0, 8)])
        S.dma("sp", H.multi(8, 8).ap, fmv(d["o1T"], 8), writes=[H.multi(8, 8)])
        S.dma("pool", BIG.multi(0, 8).ap, fmv(d["retT"], 8), writes=[BIG.multi(0, 8)])
        neglam = V(self.SM.h[:, 0:1], self.SM.tks)
        for hd in range(4):
            pss = PS.get()
            for j in range(2):
                c = hd * 2 + j
                S.stt("dve", H.t(c), H.t(8 + c), neglam, H.t(c), ALU.mult, ALU.add)
                z2 = self.SQ.t(self.sqi % 4); self.sqi += 1
                S.act(z2, H.t(c), AF.Square)
                S.mm(pss, self.ONES, z2, j == 0, j == 1)
            rstd = self.ST.t(hd % 2)
            S.ts("dve", rstd, pss, 1.0 / 256, LN_EPS, ALU.mult, ALU.add)
            S.act(rstd, rstd, AF.Sqrt)
            S.op("dve", lambda e, r=rstd: e.reciprocal(out=r.ap, in_=r.ap), reads=[rstd], writes=[rstd])
            for j in range(2):
                c = hd * 2 + j
                S.stt("dve", BIG.t(8 + c), H.t(c), self.vcol(VO["sub_g"] + c), rstd, ALU.mult, ALU.mult)
        S.dma("sp", H.multi(0, 16).ap, fmv(d["xT"], 16), writes=[H.multi(0, 16)])
        self.linear(d["w_mix_out"], KC, 0, KC, lambda k: BIG.t(k), self.resid_evac)
        self.layer_norm(VO["lng", 0, 0], VO["lnb", 0, 0])

    def conv_module(self, g, t0):
        S, PS, BIG, H, HB = self.S, self.PS, self.BIG, self.H, self.HB
        d = self.d
        UW = 32 + G
        UB = Reg(self.A, OFF_BIG, UW)
        DG = Reg(self.A, OFF_BIG + 8960, 128, stride=256)
        Y2 = Reg(self.A, OFF_BIG + 8960 + 31 * 256, G, F32)
        UT = Reg(self.A, OFF_UT, 32)
        HH = Reg(self.A, OFF_UT + 512, 32)
        w1 = d["conv_w_pw1"]
        bcol = VO["b_pw1"]
        hmask = self.vcol(VO["hmask"])
        for b in range(8):
            blka = self.WS.load(w1, 0, KC, b * 256, 256)
            blkg = self.WS.load(w1, 0, KC, D + b * 256, 256)
            for j in range(2):
                oc = b * 2 + j
                pa, pg = PS.get(), PS.get()
                for k in range(KC):
                    S.mm(pa, blka(k, j * 128, (j + 1) * 128), HB.t(k), k == 0, k == KC - 1)
                for k in range(KC):
                    S.mm(pg, blkg(k, j * 128, (j + 1) * 128), HB.t(k), k == 0, k == KC - 1)
                sg = self.ST.t(2 + (j % 2))
                S.act(sg, pg, AF.Sigmoid, bias=self.vcol(bcol + 16 + oc))
                S.stt("dve", UB.t(oc, 32, UW), pa, self.vcol(bcol + oc), sg, ALU.add, ALU.mult)
                if g == 0:
                    pa2, pg2 = PS.get(32), PS.get(32)
                    for k in range(KC):
                        S.mm(pa2, blka(k, j * 128, (j + 1) * 128), HH.t(k), k == 0, k == KC - 1)
                    for k in range(KC):
                        S.mm(pg2, blkg(k, j * 128, (j + 1) * 128), HH.t(k), k == 0, k == KC - 1)
                    sg2 = self.ST.t(3, 0, 32)
                    S.act(sg2, pg2, AF.Sigmoid, bias=self.vcol(bcol + 16 + oc))
                    S.stt("dve", sg2, pa2, self.vcol(bcol + oc), sg2, ALU.add, ALU.mult)
                    S.ts("dve", UB.t(oc, 0, 32), sg2, hmask, None, ALU.mult)
                else:
                    S.copy("dve", UB.t(oc, 0, 32), UT.t(oc))
        Y1 = Reg(self.A, OFF_HB, G, F32)
        ytile = lambda c: Y1.t(c) if c < 8 else Y2.t(c - 8)
        for c in range(KC):
            ps = PS.get()
            for k in range(31):
                dg = DG.t(k)
                S.ts("dve", dg, self.IDENT, self.vcol(VO["w_dw"] + k * 16 + c), None, ALU.mult)
                S.mm(ps, dg, UB.t(c, 2 + k, 2 + k + G), k == 0, k == 30)
            S.act(ytile(c), ps, AF.Identity, bias=self.vcol(VO["b_dw"] + c))
            S.copy("dve", UT.t(c), UB.t(c, G, UW))

        class _Y:
            t = staticmethod(lambda c, *a: ytile(c))
        CN = Reg(self.A, OFF_BIG, G)
        self.layer_norm(VO["cln_g"], VO["cln_b"], src=_Y, dstb=CN, silu=True)
        def evac(oc, ps):
            h = H.t(oc)
            tmp = self.ST.t(2 + (oc % 2))
            S.act(tmp, ps, AF.Identity, bias=self.vcol(VO["b_pw2"] + oc))
            S.stt("dve", h, h, ALPHA, tmp, ALU.mult, ALU.add)
        self.linear(d["conv_w_pw2"], KC, 0, KC, lambda k: CN.t(k), evac)
        self.layer_norm(VO["lng", 1, 0], VO["lnb", 1, 0])

    def prologue(self):
        S, PS = self.S, self.PS
        S.memset("dve", self.ONES, 1.0)
        S.memset("dve", self.ONESF, 1.0)
        S.dma("pool", self.IDENT.ap, self.d["ident"], writes=[self.IDENT])
        S.dma("sp", self.VEC.h[:], self.d["vecs"], writes=[self.VEC.all()])
        if self.do_l0:
            pr = V(self.SM.h[:, 2:4], self.SM.tks)
            S.tt("dve", V(self.SM.h[:, 2:3], self.SM.tks), self.vcol(VO["lam"]), self.vcol(VO["lam"] + 1), ALU.mult)
            S.tt("dve", V(self.SM.h[:, 3:4], self.SM.tks), self.vcol(VO["lam"] + 2), self.vcol(VO["lam"] + 3), ALU.mult)
            ps = PS.get(2)
            S.mm(ps, self.ONESF, pr, True, True)
            ex = V(self.SM.h[:, 4:6], self.SM.tks)
            S.act(ex, ps, AF.Exp)
            nl = V(self.SM.h[:, 0:1], self.SM.tks)
            S.tt("dve", nl, V(self.SM.h[:, 5:6], self.SM.tks), V(self.SM.h[:, 4:5], self.SM.tks), ALU.subtract)
            S.ts("dve", nl, nl, -LAMBDA_INIT, None, ALU.add)
            sg = self.vcol(VO["sub_g"], 8)
            S.ts("dve", sg, sg, 1.0 - LAMBDA_INIT, None, ALU.mult)
        MB = Reg(self.A, OFF_BIG, 256)
        mbv = MB.multi(0, 16)
        S.dma("pool", mbv.ap, self.d["memT"].rearrange("(k p) m -> p k m", p=128), writes=[mbv])
        for l in ([0] if self.do_l0 else []) + ([1] if self.do_l1 else []):
            self.mem_kv(l, MB)

    def run(self):
        S, H, d = self.S, self.H, self.d
        self.prologue()
        for g in range(self.NG):
            t0 = g * G
            fmv = lambda ap: ap.rearrange("(k p) t -> p k t", p=128)[:, :, t0:t0 + G]
            if self.do_l0:
                self.mix_tail(t0)
                self.xattn(0)
                self.ffn(0)
            if self.do_l1:
                if not self.do_l0:
                    S.dma("sp", H.multi(0, 16).ap, fmv(d["h1T"]), writes=[H.multi(0, 16)])
                    for c in range(KC):
                        S.act(self.HB.t(c), H.t(c), AF.Copy)
                if g == 0:
                    HH = Reg(self.A, OFF_UT + 512, 32)
                    hv = HH.multi(0, 16)
                    S.dma("pool", hv.ap, d["haloT"].rearrange("(k p) t -> p k t", p=128), writes=[hv])
                self.conv_module(g, t0)
                self.xattn(1)
                self.ffn(1)
            outname = "outT" if self.do_l1 else "h1T_out"
            self.out_dmas.append(S.dma("sp", fmv(d[outname]), H.multi(0, 16).ap, reads=[H.multi(0, 16)]))


def build_p2(NG, do_l0, do_l1):
    nc = bass.Bass("TRN2", target_bir_lowering=False)
    NT = NG * G
    d = {}

    def inp(name, shape):
        d[name] = nc.dram_tensor(name, shape, F32, kind="ExternalInput").ap()

    inp("vecs", [128, NV]); inp("ident", [128, 128]); inp("memT", [D, NMEM])
    for n in ("xattn_wq", "xattn_wk", "xattn_wv", "xattn_wo"):
        inp(n, [2, D, D])
    inp("ffn_w_in", [2, D, 2 * DFF]); inp("ffn_w_out", [2, DFF, D])
    if do_l0:
        inp("xT", [D, NT]); inp("retT", [1024, NT]); inp("o0T", [1024, NT]); inp("o1T", [1024, NT])
        inp("w_mix_out", [D, D])
    if do_l1:
        inp("conv_w_pw1", [D, 2 * D]); inp("conv_w_pw2", [D, D]); inp("haloT", [D, 32])
        if not do_l0:
            inp("h1T", [D, NT])
    outname = "outT" if do_l1 else "h1T_out"
    d[outname] = nc.dram_tensor(outname, [D, NT], F32, kind="ExternalOutput").ap()
    with ExitStack() as es:
        ar = es.enter_context(nc.sbuf_tensor("ar", [128, AR_TOTAL], BF16))
        vec = es.enter_context(nc.sbuf_tensor("vec", [128, NV], F32))
        ones = es.enter_context(nc.sbuf_tensor("ones", [128, 128], BF16))
        ident = es.enter_context(nc.sbuf_tensor("identb", [128, 128], BF16))
        onesf = es.enter_context(nc.sbuf_tensor("onesf", [128, 128], F32))
        sm = es.enter_context(nc.sbuf_tensor("sm", [128, 8], F32))
        ps = es.enter_context(nc.psum_tensor("ps", [128, 8, 512], F32))
        S = Sched(nc, es)
        A = Arena(ar, AR_TOTAL)
        consts = (V(ones[:], [Tk("ones")]), V(ident[:], [Tk("ident")]), V(onesf[:], [Tk("onesf")]), Buf(sm, 1, "sm"))
        p2 = P2(S, A, PsumPool(ps), d, Buf(vec, 1, "vec"), consts, NG, do_l0, do_l1)
        p2.run()
        S.emit(p2.out_dmas)
    return nc


TB_OFF, TB_DIAG, TB_CT, TB_DM, TB_XI, TB_SC = 0, 512, 1408, 1536, 1664, 1792
NTAB = 1796


def p1_tables(core, inp):
    h = core
    gamma = 1.0 - 2.0 ** (-5.0 - h)
    lg = np.log1p(-(2.0 ** (-5.0 - h)))
    slope = 2.0 ** (-8.0 * (core // 2 + 1) / 4)
    tab = np.zeros((128, NTAB), np.float64)
    p = np.arange(128)[:, None]
    f = np.arange(512)[None, :]
    tab[:, TB_OFF:TB_OFF + 512] = -slope * (f - p)
    xx = np.arange(-384, 512)[None, :]
    tab[:, TB_DIAG:TB_DIAG + 896] = np.where(xx - p >= 0, -slope * (xx - p), -30000.0)
    tab[:, TB_CT:TB_CT + 128] = -slope * 128.0 * np.arange(128)[None, :]
    i = np.arange(128)[None, :]
    tab[:, TB_DM:TB_DM + 128] = np.where(i - p >= 0, np.exp(lg * np.maximum(i - p, 0)), 0.0) * (128 ** -0.5)
    tab[:, TB_XI:TB_XI + 128] = np.exp(lg * (i + 1.0))
    tab[:, TB_SC + 0] = np.exp(lg * (127.0 - np.arange(128))) * (128 ** -0.5)
    tab[:, TB_SC + 1] = np.exp(lg * 128.0)
    tab[:, TB_SC + 2] = np.asarray(inp["ret_gn_g"][0, h * 128:(h + 1) * 128], np.float64)
    tab[:, TB_SC + 3] = np.asarray(inp["ret_gn_b"][0, h * 128:(h + 1) * 128], np.float64)
    return tab.astype(np.float32)


def p1_weights(core, w_in):
    h, hd, m = core, core // 2, core % 2
    sl = lambda a: w_in[:, a:a + 128]
    wf = np.concatenate([sl(h * 128), sl(1024 + h * 128), sl(3072 + h * 128),
                         sl(4096 + hd * 256 + m * 128), sl(5120 + hd * 256 + m * 128)], 1)
    wt = np.concatenate([sl(1024 + h * 128), sl(2048 + h * 128), w_in[:, 6144 + hd * 256:6144 + (hd + 1) * 256]], 1)
    return np.ascontiguousarray(wf), np.ascontiguousarray(wt)


class P1:
    parts = ('tm', 'ev1', 'ev2', 'ev3', 'ret', 'gn', 'gn2', 'att', 'att2')

    def __init__(self, S, A, PS, d, tab, ones, NTOK):
        self.S, self.A, self.PS, self.d, self.TAB, self.ONES, self.NTOK = S, A, PS, d, tab, ones, NTOK
        self.out_dmas = []

    def tb(self, c, n=1):
        return V(self.TAB.h[:, c:c + n], self.TAB.tks)

    def run(self, plan_only=False):
        S, A, PS, d, NTOK = self.S, self.A, self.PS, self.d, self.NTOK
        o = 0
        def reg(n, width, dt=BF16, stride=None):
            nonlocal o
            o = -(-o // 256) * 256
            r = Reg(A, o, width, dt, stride)
            o += n * (stride or width) * (2 if dt == F32 else 1)
            return r
        DKT = reg(1, NTOK)
        DV = reg(NTOK // 128, 257, stride=264)
        WF = reg(16, 640)
        WT = reg(16, 512)
        XB = reg(16, 512)
        RQ, RKF, DQ = reg(1, 512), reg(1, 512), reg(1, 512)
        RG, YR = reg(1, 512, F32), reg(1, 512, F32)
        RKZ, TM = reg(4, 128, stride=256), reg(4, 512)
        SMT, QX = reg(2, 128, stride=256), reg(2, 128, stride=256)
        STATE = reg(1, 128, F32)
        STATEB = reg(1, 128)
        NTT = 4
        TT = reg(NTT, 512, F32)
        PP = reg(NTT, 512)
        YB = reg(2, 512)
        ST = reg(3, 512, F32)
        OT = Reg(A, TT.base, 257, F32, stride=512)
        RL = reg(1, 8, F32)
        if plan_only:
            return -(-o // 256) * 256
        PS.nrot = 4
        acc = [PS.v(4 + q, 0, 257) for q in range(4)]
        dv_all = A.v(DV.base, (NTOK // 128) * 264)
        ones_col = V(dv_all.ap.rearrange("p (t c) -> p t c", c=264)[:, :, 256:257], dv_all.tks)
        S.memset("dve", ones_col, 1.0)

        S.memset("dve", self.ONES, 1.0)
        S.dma("sp", self.TAB.h[:], d["tab"], writes=[self.TAB.all()])
        S.dma("pool", WF.multi(0, 16).ap, d["wf"].rearrange("(k p) c -> p k c", p=128), writes=[WF.multi(0, 16)])
        S.dma("pool", WT.multi(0, 16).ap, d["wt"].rearrange("(k p) c -> p k c", p=128), writes=[WT.multi(0, 16)])
        S.memset("dve", STATE.t(0), 0.0)
        S.memset("dve", STATEB.t(0), 0.0)
        QS = 128 ** -0.5
        for g in range(NTOK // 512):
            t0 = g * 512
            xv = XB.multi(0, 16)
            S.dma("pool", xv.ap, d["xT"].rearrange("(k p) t -> p k t", p=128)[:, :, t0:t0 + 512], writes=[xv])
            for j in range(5):
                ps = PS.get()
                for k in range(16):
                    S.mm(ps, WF.t(k, j * 128, (j + 1) * 128), XB.t(k), k == 0, k == 15)
                if j == 0:
                    S.act(RQ.t(0), ps, AF.Copy)
                elif j == 1:
                    S.act(RKF.t(0), ps, AF.Copy)
                elif j == 2:
                    S.act(RG.t(0), ps, AF.Silu)
                elif j == 3:
                    S.act(DQ.t(0), ps, AF.Copy, scale=QS)
                else:
                    S.act(DKT.t(0, t0, t0 + 512), ps, AF.Copy)
            for tt in (range(4) if 'tm' in self.parts else []):
                ps = PS.get()
                k_ = PS.i - 1
                for k in range(16):
                    S.mm(ps, XB.t(k, tt * 128, (tt + 1) * 128), WT.t(k), k == 0, k == 15)
                kk = k_ % PS.nrot
                S.act(TM.t(tt), ps, AF.Copy)
                S.ts("dve", RKZ.t(tt), TM.t(tt, 0, 128), self.tb(TB_SC + 0), None, ALU.mult)
                S.copy("dve", DV.t(4 * g + tt, 0, 256), TM.t(tt, 256, 512))
            for tt in (range(4) if 'ret' in self.parts else []):
                a, b = tt * 128, (tt + 1) * 128
                pss = PS.get(128)
                S.mm(pss, RKF.t(0, a, b), RQ.t(0, a, b), True, True)
                sm = SMT.t(tt % 2); qx = QX.t(tt % 2)
                S.tt("dve", sm, pss, self.tb(TB_DM, 128), ALU.mult)
                S.tt("dve", qx, RQ.t(0, a, b), self.tb(TB_XI, 128), ALU.mult)
                psy = PS.get(128)
                S.mm(psy, TM.t(tt, 128, 256), sm, True, False)
                S.mm(psy, STATEB.t(0), qx, False, True)
                S.act(YR.t(0, a, b), psy, AF.Copy)
                pkv = PS.get(128)
                S.mm(pkv, RKZ.t(tt), TM.t(tt, 128, 256), True, True)
                S.stt("dve", STATE.t(0), STATE.t(0), self.tb(TB_SC + 1), pkv, ALU.mult, ALU.add)
                S.act(STATEB.t(0), STATE.t(0), AF.Copy)
            y = YR.t(0)
            if 'ret' not in self.parts:
                S.memset('dve', y, 0.0)
            S.act(YB.t(0), y, AF.Copy)
            S.act(YB.t(1), y, AF.Square)
            ps1, ps2 = PS.get(), PS.get()
            S.mm(ps1, self.ONES, YB.t(0), True, True)
            S.mm(ps2, self.ONES, YB.t(1), True, True)
            mean, msq, rstd = ST.t(0), ST.t(1), ST.t(2)
            S.ts("dve", mean, ps1, 1.0 / 128, None, ALU.mult)
            if 'gn' not in self.parts:
                S.memset('dve', rstd, 1.0)
            S.tt("dve", msq, mean, mean, ALU.mult)
            S.stt("dve", rstd, ps2, 1.0 / 128, msq, ALU.mult, ALU.subtract)
            S.ts("dve", rstd, rstd, 1.0, LN_EPS, ALU.mult, ALU.add)
            S.act(rstd, rstd, AF.Sqrt)
            S.op("dve", lambda e, r=rstd: e.reciprocal(out=r.ap, in_=r.ap), reads=[rstd], writes=[rstd])
            S.tt("dve", y, y, mean, ALU.subtract)
            S.tt("dve", y, y, rstd, ALU.mult)
            S.act(y, y, AF.Identity, scale=self.tb(TB_SC + 2), bias=self.tb(TB_SC + 3))
            S.tt("dve", y, y, RG.t(0), ALU.mult)
            self.out_dmas.append(S.dma("sp", d["ret_out"][:, t0:t0 + 512], y.ap, reads=[y]))
            nj = 4 * g + 4
            if 'att2' not in self.parts:
                nj = 1
            if 'att' not in self.parts:
                nj = 0
            AHEAD = 3
            pend = {}

            def score(j):
                ps = PS.get()
                S.mm(ps, DKT.t(0, j * 128, (j + 1) * 128), DQ.t(0), True, True)
                r = j - 4 * g
                bt = self.tb(TB_OFF, 512) if r < 0 else self.tb(TB_DIAG + 384 - 128 * r, 512)
                t = TT.t(j % NTT); p = PP.t(j % NTT)
                S.tt("dve", t, ps, bt, ALU.add)
                if r < 0:
                    S.act(p, t, AF.Exp, bias=self.tb(TB_CT + (4 * g - j)))
                else:
                    S.act(p, t, AF.Exp)
                pend[j] = p

            for j in range(min(AHEAD, nj)):
                score(j)
            for j in range(nj):
                if j + AHEAD < nj:
                    score(j + AHEAD)
                p = pend.pop(j)
                for q in range(4):
                    S.mm(acc[q], V(p.ap[:, q * 128:(q + 1) * 128], p.tks), DV.t(j), j == 0, j == nj - 1)
            for q in range(4):
                ot = OT.t(q)
                S.act(ot, acc[q], AF.Copy)
                rl = RL.t(0, q, q + 1)
                S.op("dve", lambda e, rl=rl, q=q: e.reciprocal(out=rl.ap, in_=OT.t(q, 256, 257).ap), reads=[OT.t(q, 256, 257)], writes=[rl])
                S.ts("dve", OT.t(q, 0, 256), OT.t(q, 0, 256), rl, None, ALU.mult)
                self.out_dmas.append(S.dma("sp", d["o_out"][t0 + q * 128:t0 + (q + 1) * 128, :], OT.t(q, 0, 256).ap, reads=[OT.t(q, 0, 256)]))


def build_p1(NTOK):
    nc = bass.Bass("TRN2", target_bir_lowering=False)
    d = {}
    for name, shape in (("xT", [D, NTOK]), ("wf", [D, 640]), ("wt", [D, 512]), ("tab", [128, NTAB])):
        d[name] = nc.dram_tensor(name, shape, F32, kind="ExternalInput").ap()
    p1o = nc.dram_tensor("p1_out", [384, NTOK], F32, kind="ExternalOutput").ap()
    d["ret_out"] = p1o[0:128, :]
    d["o_out"] = p1o[128:384, :].rearrange("r t -> (r t)").rearrange("(n c) -> n c", c=256)
    ar_total = P1(None, None, None, None, None, None, NTOK).run(plan_only=True)
    with ExitStack() as es:
        ar = es.enter_context(nc.sbuf_tensor("ar", [128, ar_total], BF16))
        tab = es.enter_context(nc.sbuf_tensor("tabs", [128, NTAB], F32))
        ones = es.enter_context(nc.sbuf_tensor("ones", [128, 128], BF16))
        ps = es.enter_context(nc.psum_tensor("ps", [128, 8, 512], F32))
        S = Sched(nc, es)
        A = Arena(ar, ar_total)
        p1 = P1(S, A, PsumPool(ps), d, Buf(tab, 1, "tab"), V(ones[:], [Tk("ones")]), NTOK)
        p1.run()
        S.emit(p1.out_dmas)
    return nc


_NC_CACHE = {}


def _get(key, fn):
    if key not in _NC_CACHE:
        _NC_CACHE[key] = fn()
    return _NC_CACHE[key]


def kernel(**inp):
    inp = {k: np.asarray(v) for k, v in inp.items()}
    cores = list(range(NCORES))
    TPC = SEQ // NCORES
    NG = TPC // G
    xT = np.ascontiguousarray(inp["x"][0].T)
    memT = np.ascontiguousarray(inp["mem"][0].T)
    ident = np.eye(128, dtype=np.float32)

    nc1 = _get("p1", lambda: build_p1(SEQ))
    maps = []
    for c in cores:
        wf, wt = p1_weights(c, inp["w_in"][0])
        maps.append({"xT": xT, "wf": wf, "wt": wt, "tab": p1_tables(c, inp)})
    r1 = run_bass_kernel_spmd(nc1, maps, core_ids=cores).results
    retT = np.concatenate([r1[c]["p1_out"][0:128] for c in cores], 0)
    otok = lambda c: r1[c]["p1_out"][128:384].reshape(SEQ, 256).T
    o0T = np.concatenate([otok(2 * h) for h in range(4)], 0)
    o1T = np.concatenate([otok(2 * h + 1) for h in range(4)], 0)
    del r1

    common = {"ident": ident, "memT": memT}
    for n in ("xattn_wq", "xattn_wk", "xattn_wv", "xattn_wo", "ffn_w_in", "ffn_w_out"):
        common[n] = inp[n]

    nc2 = _get("p2a", lambda: build_p2(NG, True, False))
    maps = []
    for c in cores:
        sl = slice(c * TPC, (c + 1) * TPC)
        m = dict(common)
        m.update(vecs=pack_vecs(inp, 1.0), xT=np.ascontiguousarray(xT[:, sl]), retT=np.ascontiguousarray(retT[:, sl]),
                 o0T=np.ascontiguousarray(o0T[:, sl]), o1T=np.ascontiguousarray(o1T[:, sl]), w_mix_out=inp["w_mix_out"][0])
        maps.append(m)
    r2 = run_bass_kernel_spmd(nc2, maps, core_ids=cores).results
    h1 = [r2[c]["h1T_out"] for c in cores]
    del r2

    nc3 = _get("p2b", lambda: build_p2(NG, False, True))
    maps = []
    for c in cores:
        m = dict(common)
        halo = h1[c - 1][:, TPC - HALO:] if c > 0 else np.zeros((D, HALO), np.float32)
        m.update(vecs=pack_vecs(inp, 1.0 if c > 0 else 0.0), h1T=h1[c], haloT=np.ascontiguousarray(halo),
                 conv_w_pw1=inp["conv_w_pw1"][0], conv_w_pw2=inp["conv_w_pw2"][0])
        maps.append(m)
    r3 = run_bass_kernel_spmd(nc3, maps, core_ids=cores).results
    outT = np.concatenate([r3[c]["outT"] for c in cores], 1)
    return np.ascontiguousarray(outT.T)[None].astype(np.float32)
```
